# Optimizing a Trainium2 kernel written in Bass

```python
import math
import jax, jax.numpy as jnp
from jax import lax
import numpy as np

D_MODEL = 2048
BATCH = 4
SEQ = 8192
DEPTH = 4

S5_WIDTH = 768
S5_GROUP = 16
S5_GROUPS = S5_WIDTH // S5_GROUP
S5_STATE = 64
SSD_HEADS = 16
SSD_HEAD_DIM = 64
SSD_WIDTH = SSD_HEADS * SSD_HEAD_DIM
SSD_GROUPS = 4
SSD_STATE = 128
SSD_CONV = 4
SSD_CHUNK = 128
SSD_CONV_CH = SSD_WIDTH + 2 * SSD_GROUPS * SSD_STATE
LRU_WIDTH = 1024
LRU_BLOCKS = 8
LRU_BLOCK = LRU_WIDTH // LRU_BLOCKS
LRU_CONV = 4
LRU_C = 8.0
RET_HEADS = 8
RET_KEY_DIM = 64
RET_VAL_DIM = 128
RET_QK_WIDTH = RET_HEADS * RET_KEY_DIM
RET_WIDTH = RET_HEADS * RET_VAL_DIM
RET_CHUNK = 128
ROPE_BASE = 10000.0
N_BRANCH = 4
D_FF = 5632
FFN_CONV = 3
N_MOD = 6
EPS = 1e-6
IN_SIZES = (S5_WIDTH, SSD_WIDTH, SSD_CONV_CH, SSD_HEADS, LRU_WIDTH, LRU_WIDTH,
            RET_QK_WIDTH, RET_QK_WIDTH, RET_WIDTH, RET_WIDTH, N_BRANCH * D_MODEL)
N_IN = sum(IN_SIZES)

kernel_name = 'hybrid_gated_parallel_mixer_trunk'


def split_cols(h, sizes):
    cuts = [int(v) for v in np.cumsum(sizes)[:-1]]
    return jnp.split(h, cuts, axis=-1)


def rmsnorm(x, w):
    xf = x.astype(jnp.float32)
    y = xf * lax.rsqrt(jnp.mean(xf * xf, axis=-1, keepdims=True) + EPS)
    return (y * w.astype(jnp.float32)).astype(x.dtype)


def causal_dwconv(x, w, b):
    k_w = w.shape[0]
    y = lax.conv_general_dilated(x, w[:, None, :].astype(x.dtype), window_strides=(1,),
                                 padding=[(k_w - 1, 0)],
                                 dimension_numbers=('NWC', 'WIO', 'NWC'),
                                 feature_group_count=x.shape[-1])
    return y + b.astype(x.dtype)


def chunk_recurrence(decay, inc):
    def step(h, xs):
        d, s = xs
        return d * h + s, h
    _, h_in = lax.scan(step, jnp.zeros_like(inc[0]), (decay, inc))
    return h_in


def apply_rope(x, positions):
    half = x.shape[-1] // 2
    inv_freq = ROPE_BASE ** (-jnp.arange(half, dtype=jnp.float32) / half)
    ang = positions.astype(jnp.float32)[..., None] * inv_freq
    cos = jnp.cos(ang)[:, :, None, :]
    sin = jnp.sin(ang)[:, :, None, :]
    x1, x2 = x[..., :half], x[..., half:]
    return jnp.concatenate([x1 * cos - x2 * sin, x1 * sin + x2 * cos], axis=-1)


def s5_mixer(u, lam_re, lam_im, log_dt, b_re, b_im, c_re, c_im, d_skip, glu_w):
    bsz, seq, _ = u.shape
    f32 = jnp.float32
    uf = u.astype(f32)
    ug = uf.reshape(bsz, seq, S5_GROUPS, S5_GROUP)
    lr = jnp.minimum(lam_re.astype(f32), -1e-4)
    li = lam_im.astype(f32)
    dt = jnp.exp(log_dt.astype(f32))[:, None]
    mag = jnp.exp(lr * dt)
    ar = mag * jnp.cos(li * dt)
    ai = mag * jnp.sin(li * dt)
    den = lr * lr + li * li
    cr = ((ar - 1.0) * lr + ai * li) / den
    ci = (ai * lr - (ar - 1.0) * li) / den
    br, bi = b_re.astype(f32), b_im.astype(f32)
    bbar_re = cr[..., None] * br - ci[..., None] * bi
    bbar_im = cr[..., None] * bi + ci[..., None] * br
    bu_re = jnp.einsum('gpc,blgc->blgp', bbar_re, ug)
    bu_im = jnp.einsum('gpc,blgc->blgp', bbar_im, ug)
    a_re = jnp.broadcast_to(ar, bu_re.shape)
    a_im = jnp.broadcast_to(ai, bu_im.shape)

    def combine(e1, e2):
        a1r, a1i, b1r, b1i = e1
        a2r, a2i, b2r, b2i = e2
        return (a2r * a1r - a2i * a1i, a2r * a1i + a2i * a1r,
                a2r * b1r - a2i * b1i + b2r, a2r * b1i + a2i * b1r + b2i)

    _, _, xr, xi = lax.associative_scan(combine, (a_re, a_im, bu_re, bu_im), axis=1)
    y = (jnp.einsum('gcp,blgp->blgc', c_re.astype(f32), xr)
         - jnp.einsum('gcp,blgp->blgc', c_im.astype(f32), xi))
    y = y.reshape(bsz, seq, S5_WIDTH) + d_skip.astype(f32) * uf
    y = jax.nn.gelu(y).astype(u.dtype)
    return y * jax.nn.sigmoid(y @ glu_w)


def ssd_mixer(z, xbc, dt_raw, conv_w, conv_b, dt_bias, a_log, d_skip, norm_w):
    bsz, seq, _ = z.shape
    f32 = jnp.float32
    nc = seq // SSD_CHUNK
    rep = SSD_HEADS // SSD_GROUPS
    xbc = jax.nn.silu(causal_dwconv(xbc, conv_w, conv_b))
    xs, bm, cm = split_cols(xbc, (SSD_WIDTH, SSD_GROUPS * SSD_STATE, SSD_GROUPS * SSD_STATE))
    dt = jax.nn.softplus(dt_raw.astype(f32) + dt_bias.astype(f32))
    a = -jnp.exp(a_log.astype(f32)).reshape(SSD_GROUPS, rep)
    xh = xs.astype(f32).reshape(bsz, nc, SSD_CHUNK, SSD_GROUPS, rep, SSD_HEAD_DIM)
    bm = bm.astype(f32).reshape(bsz, nc, SSD_CHUNK, SSD_GROUPS, SSD_STATE)
    cm = cm.astype(f32).reshape(bsz, nc, SSD_CHUNK, SSD_GROUPS, SSD_STATE)
    dt = dt.reshape(bsz, nc, SSD_CHUNK, SSD_GROUPS, rep)
    xdt = xh * dt[..., None]
    a_cs = jnp.cumsum(dt * a, axis=2)
    idx = jnp.arange(SSD_CHUNK)
    causal = (idx[:, None] >= idx[None, :])[None, None, :, :, None, None]
    seg = a_cs[:, :, :, None] - a_cs[:, :, None, :]
    decay_in = jnp.exp(jnp.where(causal, seg, -jnp.inf))
    cb = jnp.einsum('bclgn,bcsgn->bclsg', cm, bm)
    y_diag = jnp.einsum('bclsg,bclsgr,bcsgrp->bclgrp', cb, decay_in, xdt)
    decay_out = jnp.exp(a_cs[:, :, -1:] - a_cs)
    chunk_states = jnp.einsum('bcsgn,bcsgr,bcsgrp->bcgrpn', bm, decay_out, xdt)
    chunk_decay = jnp.exp(a_cs[:, :, -1])
    h_in = chunk_recurrence(jnp.moveaxis(chunk_decay, 1, 0)[..., None, None],
                            jnp.moveaxis(chunk_states, 1, 0))
    h_in = jnp.moveaxis(h_in, 0, 1)
    y_off = jnp.einsum('bclgn,bcgrpn,bclgr->bclgrp', cm, h_in, jnp.exp(a_cs))
    y = (y_diag + y_off).reshape(bsz, seq, SSD_HEADS, SSD_HEAD_DIM)
    y = y + d_skip.astype(f32)[:, None] * xs.astype(f32).reshape(bsz, seq, SSD_HEADS, SSD_HEAD_DIM)
    y = y.reshape(bsz, seq, SSD_WIDTH) * jax.nn.silu(z.astype(f32))
    return rmsnorm(y, norm_w).astype(z.dtype)


def rglru_mixer(xg, xr, conv_w, conv_b, wa, ba, wx, bx, lam):
    bsz, seq, _ = xr.shape
    f32 = jnp.float32
    gate = jax.nn.gelu(xg)
    xr = causal_dwconv(xr, conv_w, conv_b)
    xb = xr.reshape(bsz, seq, LRU_BLOCKS, LRU_BLOCK)
    r = jax.nn.sigmoid(jnp.einsum('blnc,ncd->blnd', xb, wa).reshape(bsz, seq, LRU_WIDTH) + ba)
    i = jax.nn.sigmoid(jnp.einsum('blnc,ncd->blnd', xb, wx).reshape(bsz, seq, LRU_WIDTH) + bx)
    log_a = -LRU_C * r.astype(f32) * jax.nn.softplus(-lam.astype(f32))
    a = jnp.exp(log_a)
    inp = jnp.sqrt(-jnp.expm1(2.0 * log_a)) * (i * xr).astype(f32)

    def combine(e1, e2):
        a1, b1 = e1
        a2, b2 = e2
        return a1 * a2, a2 * b1 + b2

    _, h = lax.associative_scan(combine, (a, inp), axis=1)
    return h.astype(xr.dtype) * gate


def retention_mixer(q, k, v, g, positions, gn_w):
    bsz, seq, _ = q.shape
    f32 = jnp.float32
    out_dtype = g.dtype
    nc = seq // RET_CHUNK
    q = apply_rope(q.astype(f32).reshape(bsz, seq, RET_HEADS, RET_KEY_DIM), positions)
    k = apply_rope(k.astype(f32).reshape(bsz, seq, RET_HEADS, RET_KEY_DIM), positions) * RET_KEY_DIM ** -0.5
    v = v.astype(f32).reshape(bsz, seq, RET_HEADS, RET_VAL_DIM)
    log_gamma = jnp.log1p(-jnp.exp2(-5.0 - jnp.arange(RET_HEADS, dtype=f32)))
    pos = jnp.arange(RET_CHUNK, dtype=f32)
    diff = pos[:, None] - pos[None, :]
    mask = diff >= 0
    intra = jnp.where(mask, jnp.exp(jnp.where(mask, diff, 0.0)[None] * log_gamma[:, None, None]), 0.0)
    qc = q.reshape(bsz, nc, RET_CHUNK, RET_HEADS, RET_KEY_DIM)
    kc = k.reshape(bsz, nc, RET_CHUNK, RET_HEADS, RET_KEY_DIM)
    vc = v.reshape(bsz, nc, RET_CHUNK, RET_HEADS, RET_VAL_DIM)
    scores = jnp.einsum('bcihd,bcjhd->bchij', qc, kc) * intra
    inner = jnp.einsum('bchij,bcjhe->bcihe', scores, vc)
    zeta = jnp.exp((RET_CHUNK - 1.0 - pos)[None, :] * log_gamma[:, None])
    kv = jnp.einsum('bcjhd,hj,bcjhe->bchde', kc, zeta, vc)
    chunk_decay = jnp.broadcast_to(jnp.exp(RET_CHUNK * log_gamma)[None, None, :, None, None],
                                   (nc, 1, RET_HEADS, 1, 1))
    r_in = jnp.moveaxis(chunk_recurrence(chunk_decay, jnp.moveaxis(kv, 1, 0)), 0, 1)
    xi = jnp.exp((pos + 1.0)[None, :] * log_gamma[:, None])
    cross = jnp.einsum('bcihd,bchde,hi->bcihe', qc, r_in, xi)
    o = (inner + cross).reshape(bsz, seq, RET_HEADS, RET_VAL_DIM)
    mu = jnp.mean(o, axis=-1, keepdims=True)
    var = jnp.mean(jnp.square(o - mu), axis=-1, keepdims=True)
    o = ((o - mu) * lax.rsqrt(var + EPS)).reshape(bsz, seq, RET_WIDTH) * gn_w.astype(f32)
    return (jax.nn.silu(g.astype(f32)) * o).astype(out_dtype)


def conv_ffn(h, w_up, conv_w, conv_b, w_down):
    u = causal_dwconv(h @ w_up, conv_w, conv_b)
    val, gate = jnp.split(u, 2, axis=-1)
    return (jax.nn.silu(gate) * val) @ w_down


def setup_inputs(seed: int = 0) -> dict:
    key = jax.random.key(seed)
    keys = jax.random.split(key, 48)
    ks = iter([keys[j] for j in range(48)])
    f32 = jnp.float32
    L = DEPTH

    def nrm(shape, scale):
        return scale * jax.random.normal(next(ks), shape, f32)

    def unif(shape, lo, hi):
        return jax.random.uniform(next(ks), shape, f32, lo, hi)

    x = nrm((BATCH, SEQ, D_MODEL), 1.0)
    c = nrm((BATCH, D_MODEL), 1.0)
    offs = jax.random.randint(next(ks), (BATCH, 1), 0, 1024, jnp.int32)
    positions = (jnp.arange(SEQ, dtype=jnp.int32)[None, :] + offs).astype(jnp.int32)
    ada_w = nrm((L, D_MODEL, N_MOD * D_MODEL), 0.3 * D_MODEL ** -0.5)
    ada_b = nrm((L, N_MOD * D_MODEL), 0.02)
    norm1_w = 1.0 + nrm((L, D_MODEL), 0.02)
    norm2_w = 1.0 + nrm((L, D_MODEL), 0.02)
    w_in = nrm((L, D_MODEL, N_IN), D_MODEL ** -0.5)
    s5_lam_re = -0.5 + nrm((L, S5_GROUPS, S5_STATE), 0.01)
    s5_lam_im = math.pi * jnp.arange(S5_STATE, dtype=f32) + nrm((L, S5_GROUPS, S5_STATE), 0.01)
    s5_log_dt = unif((L, S5_GROUPS), math.log(1e-3), math.log(1e-1))
    s5_b_re = nrm((L, S5_GROUPS, S5_STATE, S5_GROUP), (2.0 * S5_GROUP) ** -0.5)
    s5_b_im = nrm((L, S5_GROUPS, S5_STATE, S5_GROUP), (2.0 * S5_GROUP) ** -0.5)
    s5_c_re = nrm((L, S5_GROUPS, S5_GROUP, S5_STATE), (2.0 * S5_STATE) ** -0.5)
    s5_c_im = nrm((L, S5_GROUPS, S5_GROUP, S5_STATE), (2.0 * S5_STATE) ** -0.5)
    s5_d = nrm((L, S5_WIDTH), 0.5)
    s5_glu_w = nrm((L, S5_WIDTH, S5_WIDTH), S5_WIDTH ** -0.5)
    ssd_conv_w = nrm((L, SSD_CONV, SSD_CONV_CH), SSD_CONV ** -0.5)
    ssd_conv_b = nrm((L, SSD_CONV_CH), 0.01)
    dt0 = jnp.exp(unif((L, SSD_HEADS), math.log(1e-3), math.log(1e-1)))
    ssd_dt_bias = dt0 + jnp.log(-jnp.expm1(-dt0))
    ssd_a_log = jnp.log(unif((L, SSD_HEADS), 1.0, 16.0))
    ssd_d = 1.0 + nrm((L, SSD_HEADS), 0.1)
    ssd_norm_w = 1.0 + nrm((L, SSD_WIDTH), 0.02)
    lru_conv_w = nrm((L, LRU_CONV, LRU_WIDTH), LRU_CONV ** -0.5)
    lru_conv_b = nrm((L, LRU_WIDTH), 0.01)
    lru_wa = nrm((L, LRU_BLOCKS, LRU_BLOCK, LRU_BLOCK), LRU_BLOCK ** -0.5)
    lru_ba = nrm((L, LRU_WIDTH), 0.01)
    lru_wx = nrm((L, LRU_BLOCKS, LRU_BLOCK, LRU_BLOCK), LRU_BLOCK ** -0.5)
    lru_bx = nrm((L, LRU_WIDTH), 0.01)
    a0 = unif((L, LRU_WIDTH), 0.9, 0.999) ** (1.0 / LRU_C)
    lru_lambda = jnp.log(a0) - jnp.log1p(-a0)
    ret_gn_w = 1.0 + nrm((L, RET_WIDTH), 0.02)
    w_br_a = nrm((L, S5_WIDTH, D_MODEL), S5_WIDTH ** -0.5)
    w_br_b = nrm((L, SSD_WIDTH, D_MODEL), SSD_WIDTH ** -0.5)
    w_br_c = nrm((L, LRU_WIDTH, D_MODEL), LRU_WIDTH ** -0.5)
    w_br_d = nrm((L, RET_WIDTH, D_MODEL), RET_WIDTH ** -0.5)
    w_out = nrm((L, D_MODEL, D_MODEL), D_MODEL ** -0.5)
    ffn_w_up = nrm((L, D_MODEL, 2 * D_FF), D_MODEL ** -0.5)
    ffn_conv_w = nrm((L, FFN_CONV, 2 * D_FF), FFN_CONV ** -0.5)
    ffn_conv_b = nrm((L, 2 * D_FF), 0.01)
    ffn_w_down = nrm((L, D_FF, D_MODEL), D_FF ** -0.5)
    final_norm_w = 1.0 + nrm((D_MODEL,), 0.02)
    return {'x': x, 'c': c, 'positions': positions, 'ada_w': ada_w, 'ada_b': ada_b,
            'norm1_w': norm1_w, 'norm2_w': norm2_w, 'w_in': w_in,
            's5_lam_re': s5_lam_re, 's5_lam_im': s5_lam_im, 's5_log_dt': s5_log_dt,
            's5_b_re': s5_b_re, 's5_b_im': s5_b_im, 's5_c_re': s5_c_re, 's5_c_im': s5_c_im,
            's5_d': s5_d, 's5_glu_w': s5_glu_w,
            'ssd_conv_w': ssd_conv_w, 'ssd_conv_b': ssd_conv_b, 'ssd_dt_bias': ssd_dt_bias,
            'ssd_a_log': ssd_a_log, 'ssd_d': ssd_d, 'ssd_norm_w': ssd_norm_w,
            'lru_conv_w': lru_conv_w, 'lru_conv_b': lru_conv_b, 'lru_wa': lru_wa, 'lru_ba': lru_ba,
            'lru_wx': lru_wx, 'lru_bx': lru_bx, 'lru_lambda': lru_lambda,
            'ret_gn_w': ret_gn_w,
            'w_br_a': w_br_a, 'w_br_b': w_br_b, 'w_br_c': w_br_c, 'w_br_d': w_br_d, 'w_out': w_out,
            'ffn_w_up': ffn_w_up, 'ffn_conv_w': ffn_conv_w, 'ffn_conv_b': ffn_conv_b,
            'ffn_w_down': ffn_w_down, 'final_norm_w': final_norm_w}


def reference(x, c, positions, ada_w, ada_b, norm1_w, norm2_w, w_in,
              s5_lam_re, s5_lam_im, s5_log_dt, s5_b_re, s5_b_im, s5_c_re, s5_c_im, s5_d, s5_glu_w,
              ssd_conv_w, ssd_conv_b, ssd_dt_bias, ssd_a_log, ssd_d, ssd_norm_w,
              lru_conv_w, lru_conv_b, lru_wa, lru_ba, lru_wx, lru_bx, lru_lambda,
              ret_gn_w, w_br_a, w_br_b, w_br_c, w_br_d, w_out,
              ffn_w_up, ffn_conv_w, ffn_conv_b, ffn_w_down, final_norm_w):
    bsz, seq = x.shape[0], x.shape[1]
    for l in range(DEPTH):
        mod = c @ ada_w[l] + ada_b[l]
        sh1, sc1, g1, sh2, sc2, g2 = [m[:, None, :] for m in jnp.split(mod, N_MOD, axis=-1)]
        h = rmsnorm(x, norm1_w[l]) * (1.0 + sc1) + sh1
        proj = h @ w_in[l]
        (u_s5, z_ssd, xbc_ssd, dt_ssd, xg_lru, xr_lru,
         q_ret, k_ret, v_ret, g_ret, gate_logits) = split_cols(proj, IN_SIZES)
        y_a = s5_mixer(u_s5, s5_lam_re[l], s5_lam_im[l], s5_log_dt[l], s5_b_re[l], s5_b_im[l],
                       s5_c_re[l], s5_c_im[l], s5_d[l], s5_glu_w[l])
        y_b = ssd_mixer(z_ssd, xbc_ssd, dt_ssd, ssd_conv_w[l], ssd_conv_b[l], ssd_dt_bias[l],
                        ssd_a_log[l], ssd_d[l], ssd_norm_w[l])
        y_c = rglru_mixer(xg_lru, xr_lru, lru_conv_w[l], lru_conv_b[l], lru_wa[l], lru_ba[l],
                          lru_wx[l], lru_bx[l], lru_lambda[l])
        y_d = retention_mixer(q_ret, k_ret, v_ret, g_ret, positions, ret_gn_w[l])
        gates = jax.nn.sigmoid(gate_logits.reshape(bsz, seq, N_BRANCH, D_MODEL))
        merged = (gates[:, :, 0] * (y_a @ w_br_a[l]) + gates[:, :, 1] * (y_b @ w_br_b[l])
                  + gates[:, :, 2] * (y_c @ w_br_c[l]) + gates[:, :, 3] * (y_d @ w_br_d[l]))
        x = x + g1 * (merged @ w_out[l])
        h = rmsnorm(x, norm2_w[l]) * (1.0 + sc2) + sh2
        x = x + g2 * conv_ffn(h, ffn_w_up[l], ffn_conv_w[l], ffn_conv_b[l], ffn_w_down[l])
    return rmsnorm(x, final_norm_w)
```

```python
import math, contextlib
import numpy as np
import concourse.bass as bass
import concourse.mybir as mybir
from concourse.bass_utils import run_bass_kernel_spmd

F32 = mybir.dt.float32
BF16 = mybir.dt.bfloat16
I32 = mybir.dt.int32
AF = mybir.ActivationFunctionType
ALU = mybir.AluOpType
AX = mybir.AxisListType
ENGS = ("pe", "act", "dve", "pool", "sp")


class Op:
    __slots__ = ("eng", "fn", "reads", "writes", "dma", "seq", "waits", "marked", "dsem", "dcount", "mval")

    def __init__(self, eng, fn, reads, writes, dma):
        self.eng, self.fn, self.reads, self.writes, self.dma = eng, fn, reads, writes, dma
        self.waits = []
        self.marked = False
        self.dsem = None
        self.dcount = 0


class Sched:
    def __init__(self, nc, n_dma_sems=8):
        self.nc = nc
        self.ops = []
        self.n_dma_sems = n_dma_sems

    def op(self, eng, fn, reads=(), writes=(), dma=False):
        o = Op(eng, fn, tuple(reads), tuple(writes), dma)
        self.ops.append(o)
        return o

    def dma(self, q, out, in_, reads=(), writes=(), **kw):
        return self.op(q, lambda e: e.dma_start(out=out, in_=in_, **kw), reads, writes, dma=True)

    def finalize(self, sems):
        ops = self.ops
        cnt = {e: 0 for e in ENGS}
        for o in ops:
            cnt[o.eng] += 1
            o.seq = cnt[o.eng]
        dma_rr = {e: 0 for e in ENGS}
        dma_cnt = {}
        last_w = {}
        readers = {}
        seen = {e: {} for e in ENGS}
        dma_last = {}

        def need(o, tgt):
            if tgt is None:
                return
            kind, stream, val, top = tgt
            if kind == 'e' and stream == o.eng and not o.dma and o.eng == 'pe':
                return
            s = seen[o.eng]
            key = (kind, stream)
            if s.get(key, 0) >= val:
                return
            s[key] = val
            o.waits.append(tgt)
            if kind == 'e':
                top.marked = True

        for o in ops:
            if o.dma:
                q = o.eng
                sid = (q, dma_rr[q] % self.n_dma_sems)
                dma_rr[q] += 1
                need(o, dma_last.get(sid))
                dma_cnt[sid] = dma_cnt.get(sid, 0) + 1
                o.dsem = sid
                o.dcount = dma_cnt[sid]
                me = ('d', sid, o.dcount * 16, o)
                dma_last[sid] = me
            else:
                me = ('e', o.eng, o.seq, o)
            for k in o.reads:
                need(o, last_w.get(k))
            for k in o.writes:
                need(o, last_w.get(k))
                for t in readers.get(k, {}).values():
                    need(o, t)
            for k in o.reads:
                readers.setdefault(k, {})[me[1]] = me
            for k in o.writes:
                last_w[k] = me
                readers[k] = {}
        mcount = {e: 0 for e in ENGS}
        for o in ops:
            if o.marked:
                mcount[o.eng] += 1
                o.mval = mcount[o.eng]
        self.final_dma = dict(dma_last)
        self.sems = sems

    def emit_engine(self, eng_name, e):
        sems = self.sems
        for o in self.ops:
            if o.eng != eng_name:
                continue
            for (kind, stream, val, top) in o.waits:
                if kind == 'e':
                    e.wait_ge(sems["sem_" + stream], top.mval)
                else:
                    e.wait_ge(sems["dma_%s_%d" % stream], val)
            ins = o.fn(e)
            if o.dma:
                ins.then_inc(sems["dma_%s_%d" % o.dsem], 16)
            elif o.marked:
                ins.then_inc(sems["sem_" + o.eng], 1)
        for sid, (kind, stream, val, top) in self.final_dma.items():
            if sid[0] == eng_name:
                e.wait_ge(sems["dma_%s_%d" % sid], val)


class Tile:
    _n = 0

    def __init__(self, nc, shape, dtype, space="sbuf", name=None, nsub=1):
        Tile._n += 1
        self.name = name or ("t%d" % Tile._n)
        if space == "sbuf":
            self.h = nc.alloc_sbuf_tensor(self.name, list(shape), dtype)
        else:
            self.h = nc.alloc_psum_tensor(self.name, list(shape), dtype)
        self.ap = self.h.ap()
        self.nsub = nsub

    def k(self, i=None):
        if i is None:
            return [(self.name, j) for j in range(self.nsub)]
        if isinstance(i, (list, tuple, range)):
            return [(self.name, j) for j in i]
        return [(self.name, i)]

    def __getitem__(self, idx):
        return self.ap[idx]


class View:
    def __init__(self, arena, off, nbytes, dtype, pattern=None, **dims):
        a = arena.ap[:, off // 2:(off + nbytes) // 2]
        if dtype != BF16:
            a = a.bitcast(dtype)
        if pattern:
            a = a.rearrange(pattern, **dims)
        self.ap = a
        self.keys = [(arena.name, j) for j in range(off // 1024, (off + nbytes + 1023) // 1024)]

    def k(self, i=None):
        return self.keys

    def __getitem__(self, idx):
        return self.ap[idx]


D = 2048
NIN = 17168
DFF = 5632
TT = 512
EPS = 1e-6
O_U, O_Z, O_XBC, O_DT, O_XG, O_XR, O_Q, O_K, O_V, O_G, O_GATE = 0, 768, 1792, 3840, 3856, 4880, 5904, 6416, 6928, 7952, 8976
PI = math.pi
V_N1, V_N2, V_ADAB, V_SCW, V_SCB, V_LCW, V_LCB, V_LBA, V_LBX, V_LLAM, V_FCW, V_FCB, V_S5D, V_SNW, V_GNW, NV = \
    0, 16, 32, 128, 192, 208, 240, 248, 256, 264, 272, 536, 624, 630, 638, 646

WSHAPES = {
    'ada_w': (D, 6 * D), 'ada_b': (6 * D,), 'norm1_w': (D,), 'norm2_w': (D,), 'w_in': (D, NIN),
    's5_lam_re': (48, 64), 's5_lam_im': (48, 64), 's5_log_dt': (48,), 's5_b_re': (48, 64, 16), 's5_b_im': (48, 64, 16),
    's5_c_re': (48, 16, 64), 's5_c_im': (48, 16, 64), 's5_d': (768,), 's5_glu_w': (768, 768),
    'ssd_conv_w': (4, 2048), 'ssd_conv_b': (2048,), 'ssd_dt_bias': (16,), 'ssd_a_log': (16,), 'ssd_d': (16,),
    'ssd_norm_w': (1024,), 'lru_conv_w': (4, 1024), 'lru_conv_b': (1024,), 'lru_wa': (8, 128, 128), 'lru_ba': (1024,),
    'lru_wx': (8, 128, 128), 'lru_bx': (1024,), 'lru_lambda': (1024,), 'ret_gn_w': (1024,),
    'w_br_a': (768, D), 'w_br_b': (1024, D), 'w_br_c': (1024, D), 'w_br_d': (1024, D), 'w_out': (D, D),
    'ffn_w_up': (D, 2 * DFF), 'ffn_conv_w': (3, 2 * DFF), 'ffn_conv_b': (2 * DFF,), 'ffn_w_down': (DFF, D),
}
CAST = ['w_in', 's5_glu_w', 'lru_wa', 'lru_wx', 'w_br_a', 'w_br_b', 'w_br_c', 'w_br_d', 'w_out', 'ffn_w_up', 'ffn_w_down']


def host_consts():
    c = {}
    c['ident'] = np.eye(128, dtype=np.float32)
    idx = np.arange(128)
    c['triu'] = (idx[:, None] <= idx[None, :]).astype(np.float32)
    c['negm'] = np.where(idx[None, :] >= idx[:, None], 0.0, -30000.0).astype(np.float32)
    lg = np.log1p(-np.exp2(-5.0 - np.arange(8, dtype=np.float64)))
    diff = idx[None, :] - idx[:, None]
    intra = np.where(diff >= 0, np.exp(np.maximum(diff, 0)[None] * lg[:, None, None]), 0.0)
    c['intraT'] = np.ascontiguousarray(intra.transpose(1, 0, 2)).reshape(128, 1024).astype(np.float32)
    c['zeta'] = np.exp((127.0 - idx)[:, None] * lg[None, :]).astype(np.float32)
    xi = np.exp((idx + 1.0)[None, :] * lg[:, None])
    xit = np.zeros((128, 4, 128), np.float32)
    gdec = np.zeros((128, 4), np.float32)
    for t in range(4):
        for hh in range(2):
            xit[hh * 64:(hh + 1) * 64, t, :] = xi[2 * t + hh][None, :]
            gdec[hh * 64:(hh + 1) * 64, t] = np.exp(128.0 * lg[2 * t + hh])
    c['xit'] = xit.reshape(128, 512)
    c['gdec'] = gdec
    r = np.arange(128)
    f = (r % 64) % 32
    invf = (10000.0 ** (-f.astype(np.float64) / 32.0)).astype(np.float32)
    sign = np.where((r % 64) < 32, -1.0, 1.0).astype(np.float32)
    c['rc'] = np.stack([invf, sign], axis=1).astype(np.float32)
    perm = np.zeros((128, 128), np.float32)
    for i in range(128):
        perm[(i // 64) * 64 + ((i % 64) + 32) % 64, i] = 1.0
    c['perm'] = perm
    hm = np.zeros((128, 2), np.float32); hm[:64, 0] = 1.0; hm[64:, 1] = 1.0
    c['hmask'] = hm
    mb = np.zeros((128, 4, 8, 16), np.float32)
    mc = np.zeros((128, 4, 128), np.float32)
    for q in range(128):
        h = q // 64
        for j in range(4):
            mb[q, j, 2 * j + h, :] = 1.0
            mc[q, j, (2 * j + h) * 16:(2 * j + h + 1) * 16] = 1.0
    c['maskb'] = mb.reshape(128, 512)
    c['maskc'] = mc.reshape(128, 512)
    return c


def build(S, L, dbg=(), stop=None, final_norm=True):
    nc = bass.Bass("TRN2", target_bir_lowering=False)
    NTL = S // TT
    sch = Sched(nc)
    Tile._n = 0

    def din(n, shp, dt=F32):
        return nc.dram_tensor(n, list(shp), dt, kind="ExternalInput").ap()

    def dscr(n, shp, dt):
        return nc.dram_tensor(n, list(shp), dt, kind="Internal").ap()

    x_d = din("x", [S, D])
    c_d = din("c", [16, 128])
    pos_d = din("positions", [S], I32)
    W = {n: din(n, (L,) + s) for n, s in WSHAPES.items()} if L > 0 else {}
    fnw_d = din("final_norm_w", [16, 128])
    hc = host_consts()
    CD = {n: din("k_" + n, v.shape) for n, v in hc.items()}
    out_d = nc.dram_tensor("out", [S, D], F32, kind="ExternalOutput").ap()
    dbg_d = {n: nc.dram_tensor("dbg_" + n, list(shp), (BF16 if n in ("h1", "ya", "yb", "yc", "yd") else F32), kind="ExternalOutput").ap() for n, shp in dbg}
    WB = {n: [dscr("b_%s_%d" % (n, l_), WSHAPES[n], BF16) for l_ in range(L)] for n in CAST}
    XD = dscr("xd", [16, 128, S], F32)
    ROPD = dscr("ropd", [2, 128, S], F32)
    S5WB = dscr("s5wb", [max(L, 1), 128, 24, 2, 128], BF16)
    S5WC = dscr("s5wc", [max(L, 1), 128, 24, 2, 128], BF16)

    XT = Tile(nc, [128, 16, TT], F32, name="XT")
    HT = Tile(nc, [128, 16, TT], BF16, name="HT")
    WS = [Tile(nc, [128, 8192], BF16, name="WS%d" % i) for i in range(2)]
    YA = Tile(nc, [128, 6, TT], BF16, name="YA")
    YB = Tile(nc, [128, 8, TT], BF16, name="YB")
    YC = Tile(nc, [128, 8, TT], BF16, name="YC")
    YD = Tile(nc, [128, 8, TT], BF16, name="YD")
    AR = Tile(nc, [128, 33792], BF16, name="AR", nsub=66)
    PS = Tile(nc, [128, 8, 512], F32, space="psum", name="PS", nsub=8)
    IDF = Tile(nc, [128, 128], F32, name="IDF")
    IDB = Tile(nc, [128, 128], BF16, name="IDB")
    ONB = Tile(nc, [128, 128], BF16, name="ONB")
    ONF = Tile(nc, [128, 128], F32, name="ONF")
    TRI = Tile(nc, [128, 128], F32, name="TRI")
    NEGM = Tile(nc, [128, 128], F32, name="NEGM")
    INTRA = Tile(nc, [128, 8, 128], F32, name="INTRA")
    ZETA = Tile(nc, [128, 8], F32, name="ZETA")
    XIT = Tile(nc, [128, 4, 128], F32, name="XIT")
    GDEC = Tile(nc, [128, 4], F32, name="GDEC")
    RC = Tile(nc, [128, 2], F32, name="RC")
    HM = Tile(nc, [128, 2], F32, name="HM")
    PERM = Tile(nc, [128, 128], BF16, name="PERM")
    CONE = Tile(nc, [128, 1], F32, name="CONE")
    CEPS = Tile(nc, [128, 1], F32, name="CEPS")
    CT = Tile(nc, [128, 16], F32, name="CT")
    FNW = Tile(nc, [128, 16], F32, name="FNW")
    ROPC = Tile(nc, [128, TT], F32, name="ROPC")
    ROPS = Tile(nc, [128, TT], F32, name="ROPS")
    RSTD = Tile(nc, [128, TT], F32, name="RSTD")
    TMPF = [View(AR, 57344 + i * 2048, 2048, F32) for i in range(2)]
    SQB = [View(AR, 61440 + i * 1024, 1024, BF16) for i in range(2)]
    VEC = Tile(nc, [128, NV], F32, name="VEC")
    MOD = Tile(nc, [128, 96], F32, name="MOD")
    WE = Tile(nc, [128, 2, 16], F32, name="WE")
    S5S = Tile(nc, [128, 4, 24], F32, name="S5S")
    PW = Tile(nc, [128, 3, 24, 9], F32, name="PW")
    DIAGD = Tile(nc, [128, 6, 128], BF16, name="DIAGD")
    CAR = Tile(nc, [128, 2, 24], F32, name="CAR")
    HL = Tile(nc, [128, 8], F32, name="HL")
    HALS = Tile(nc, [128, 16, 3], F32, name="HALS")
    HALL = Tile(nc, [128, 8, 3], F32, name="HALL")
    FH = Tile(nc, [128, 88, 2], F32, name="FH")
    HS = Tile(nc, [128, 1024], F32, name="HS")
    RS = Tile(nc, [128, 4, 128], F32, name="RS")
    C8 = Tile(nc, [128, 8], F32, name="C8")
    ROWS = Tile(nc, [128, 3, 16], F32, name="ROWS")

    marks = {}

    def mark(n):
        marks.setdefault(n, len(sch.ops))

    psc = [0]

    def ps_rot():
        psc[0] = (psc[0] + 1) % 4
        return psc[0]

    def P(i):
        return PS.ap[:, i, :]

    def act(out, in_, func, r, w, **kw):
        sch.op("act", lambda e: e.activation(out=out, in_=in_, func=func, **kw), r, w)

    def tt(out, in0, in1, op, r, w, eng="dve"):
        sch.op(eng, lambda e: e.tensor_tensor(out=out, in0=in0, in1=in1, op=op), r, w)

    def stt(out, in0, scalar, in1, op0, op1, r, w, eng="dve"):
        sch.op(eng, lambda e: e.scalar_tensor_tensor(out=out, in0=in0, scalar=scalar, in1=in1, op0=op0, op1=op1), r, w)

    def ts(out, in0, s1, s2, op0, op1, r, w, eng="dve"):
        if s2 is None:
            sch.op(eng, lambda e: e.tensor_scalar(out=out, in0=in0, scalar1=s1, scalar2=None, op0=op0), r, w)
        else:
            sch.op(eng, lambda e: e.tensor_scalar(out=out, in0=in0, scalar1=s1, scalar2=s2, op0=op0, op1=op1), r, w)

    def cp(out, in_, r, w, eng="dve"):
        sch.op(eng, lambda e: e.tensor_copy(out=out, in_=in_), r, w)

    def mm(out, lhsT, rhs, start, stop, r, w):
        sch.op("pe", lambda e: e.matmul(out, lhsT=lhsT, rhs=rhs, start=start, stop=stop), r, w)

    def tr(out, in_, ident, r, w):
        sch.op("pe", lambda e: e.transpose(out=out, in_=in_, identity=ident), r, w)

    def memset(tile_ap, val, w, eng="dve"):
        sch.op(eng, lambda e: e.memset(tile_ap, val), (), w)

    def dbgout(name, ap, r):
        if name in dbg_d:
            sch.dma("sp", dbg_d[name], ap, reads=r)

    for l in range(L):
        for n in CAST:
            ne = int(np.prod(WSHAPES[n]))
            rows = ne // 2048
            pat = " ".join("abc"[:len(WSHAPES[n])])
            src = W[n][l].rearrange("%s -> (%s)" % (pat, pat)).rearrange("(r c) -> r c", c=2048)
            dst = WB[n][l].rearrange("%s -> (%s)" % (pat, pat)).rearrange("(r c) -> r c", c=2048)
            for r0 in range(0, rows, 4096):
                r1 = min(rows, r0 + 4096)
                sch.dma("pool", dst[r0:r1], src[r0:r1], reads=(), writes=[("wb", n, l)])

    mark('casts')
    def ldc(tile, name, shape_pat=None, **dims):
        src = CD[name]
        dst = tile.ap
        if shape_pat:
            src = src.rearrange(shape_pat, **dims)
        sch.dma("sp", dst, src, writes=tile.k())

    ldc(IDF, 'ident'); ldc(TRI, 'triu'); ldc(NEGM, 'negm')
    ldc(INTRA, 'intraT', "p (h i) -> p h i", h=8)
    ldc(ZETA, 'zeta'); ldc(XIT, 'xit', "p (t i) -> p t i", t=4); ldc(GDEC, 'gdec'); ldc(RC, 'rc'); ldc(HM, 'hmask')
    cp(IDB.ap, IDF.ap, IDF.k(), IDB.k())
    memset(ONB.ap, 1.0, ONB.k()); memset(ONF.ap, 1.0, ONF.k())
    memset(CONE.ap, 1.0, CONE.k()); memset(CEPS.ap, EPS, CEPS.k())
    PERMF = View(AR, 0, 512, F32)
    sch.dma("sp", PERMF.ap, CD['perm'], writes=PERMF.k())
    cp(PERM.ap, PERMF.ap, PERMF.k(), PERM.k())
    MASKB = View(AR, 1024, 2048, F32, "p (j g c) -> p j g c", j=4, g=8)
    MASKC = View(AR, 3072, 2048, F32, "p (j c) -> p j c", j=4)
    STG = View(AR, 8192, 8192, F32)
    STG2 = View(AR, 16384, 512, F32)

    def vec_cols(dst_ap, src_rows_ap, n, wkeys):
        sch.dma("sp", STG2.ap[:n, :], src_rows_ap, writes=STG2.k())
        b = ps_rot()
        tr(P(b)[:, 0:n], STG2.ap[:n, :], IDF.ap[:n, :n], STG2.k() + IDF.k(), PS.k(b))
        cp(dst_ap, P(b)[:, 0:n], PS.k(b), wkeys)

    vec_cols(CT.ap, c_d, 16, CT.k())
    vec_cols(FNW.ap, fnw_d, 16, FNW.k())

    def sincos(ang_ap, n, sin_out, cos_out, tA, tI, tB, kA, kI, kB, r, w_sin, w_cos):
        for shift, outap, wk in ((0.0, sin_out, w_sin), (PI / 2, cos_out, w_cos)):
            ts(tA, ang_ap, shift, None, ALU.add, None, r, kA)
            ts(tI, tA, 1.0 / (2 * PI), None, ALU.mult, None, kA, kI)
            cp(tB, tI, kI, kB)
            stt(tA, tB, -2 * PI, tA, ALU.mult, ALU.add, kB + kA, kA)
            ts(tA, tA, -3.14159, 3.14159, ALU.max, ALU.min, kA, kA)
            act(outap, tA, AF.Sin, kA, wk)

    mark('consts')
    for t in range(NTL):
        t0 = t * TT
        for c in range(4):
            sch.dma("sp", STG.ap, x_d[t0 + c * 128:t0 + (c + 1) * 128, :], writes=STG.k())
            for f0 in range(0, 16, 4):
                b = ps_rot()
                for j in range(4):
                    tr(P(b)[:, j * 128:(j + 1) * 128], STG.ap[:, (f0 + j) * 128:(f0 + j + 1) * 128], IDF.ap, STG.k() + IDF.k(), PS.k(b))
                cp(XT.ap[:, f0:f0 + 4, c * 128:(c + 1) * 128], P(b).rearrange("p (j i) -> p j i", j=4), PS.k(b), XT.k(),
                   eng="dve")
        sch.dma("sp", XD[:, :, t0:t0 + TT].rearrange("f p t -> p f t"), XT.ap, reads=XT.k(), writes=[("xd", t)])
        POSI = View(AR, 17408, 2048, I32); POSF = View(AR, 19456, 2048, F32); ANG = View(AR, 21504, 2048, F32)
        TA = View(AR, 23552, 2048, F32); TI = View(AR, 25600, 2048, I32); TB = View(AR, 27648, 2048, F32)
        SINT = View(AR, 29696, 2048, F32)
        sch.dma("sp", POSI.ap, pos_d[t0:t0 + TT].partition_broadcast(128), writes=POSI.k())
        cp(POSF.ap, POSI.ap, POSI.k(), POSF.k())
        ts(ANG.ap, POSF.ap, RC.ap[:, 0:1], None, ALU.mult, None, POSF.k() + RC.k(), ANG.k())
        sincos(ANG.ap, TT, SINT.ap, ROPC.ap, TA.ap, TI.ap, TB.ap, TA.k(), TI.k(), TB.k(), ANG.k(), SINT.k(), ROPC.k())
        ts(ROPS.ap, SINT.ap, RC.ap[:, 1:2], None, ALU.mult, None, SINT.k() + RC.k(), ROPS.k())
        sch.dma("sp", ROPD[0, :, t0:t0 + TT], ROPC.ap, reads=ROPC.k(), writes=[("ropd", t)])
        sch.dma("sp", ROPD[1, :, t0:t0 + TT], ROPS.ap, reads=ROPS.k(), writes=[("ropd", t)])

    mark('pass0')
    wsc = [0]

    def load_slab(src_ap, kcn, ncols, rkeys):
        wsc[0] = (wsc[0] + 1) % 2
        s = WS[wsc[0]]
        v = s.ap[:, 0:kcn * ncols].rearrange("p (k c) -> p k c", k=kcn)
        sch.dma("sp", v, src_ap.rearrange("(k p) c -> p k c", p=128), reads=rkeys, writes=s.k())
        return s, v

    def norm_mod(wcol, shcol, r_extra):
        b = ps_rot()
        for fc in range(16):
            sq = SQB[fc % 2]
            act(sq.ap, XT.ap[:, fc, :], AF.Square, XT.k(), sq.k())
            mm(P(b), ONB.ap, sq.ap, fc == 0, fc == 15, ONB.k() + sq.k(), PS.k(b))
        act(RSTD.ap, P(b), AF.Sqrt, PS.k(b) + CEPS.k(), RSTD.k(), scale=1.0 / D, bias=CEPS.ap)
        sch.op("dve", lambda e: e.reciprocal(out=RSTD.ap, in_=RSTD.ap), RSTD.k(), RSTD.k())
        for fc in range(16):
            tf = TMPF[fc % 2]
            stt(tf.ap, XT.ap[:, fc, :], wcol(fc), RSTD.ap, ALU.mult, ALU.mult, XT.k() + RSTD.k() + r_extra, tf.k())
            if shcol is None:
                act(HT.ap[:, fc, :], tf.ap, AF.Copy, tf.k(), HT.k())
            else:
                act(HT.ap[:, fc, :], tf.ap, AF.Identity, tf.k() + r_extra, HT.k(), bias=shcol(fc))

    def proj_fm(l, col0, ntiles, consume):
        for c0 in range(0, ntiles, 4):
            n = min(4, ntiles - c0)
            s, v = load_slab(WB['w_in'][l][:, col0 + c0 * 128: col0 + (c0 + n) * 128], 16, n * 128, [("wb", 'w_in', l)])
            for j in range(n):
                b = ps_rot()
                for kc in range(16):
                    mm(P(b), v[:, kc, j * 128:(j + 1) * 128], HT.ap[:, kc, :], kc == 0, kc == 15, s.k() + HT.k(), PS.k(b))
                consume(c0 + j, b)

    def proj_tm(l, col0, ncols, consume):
        s, v = load_slab(WB['w_in'][l][:, col0:col0 + ncols], 16, ncols, [("wb", 'w_in', l)])
        for c in range(4):
            b = ps_rot()
            for kc in range(16):
                mm(P(b)[:, 0:ncols], HT.ap[:, kc, c * 128:(c + 1) * 128], v[:, kc, :], kc == 0, kc == 15, s.k() + HT.k(), PS.k(b))
            consume(c, b)

    def emit_output(t, do_norm):
        t0 = t * TT
        STGo = View(AR, 0, 8192, F32)
        b = ps_rot() if do_norm else None
        for fc in (range(16) if do_norm else ()):
            sq = SQB[fc % 2]
            act(sq.ap, XT.ap[:, fc, :], AF.Square, XT.k(), sq.k())
            mm(P(b), ONB.ap, sq.ap, fc == 0, fc == 15, ONB.k() + sq.k(), PS.k(b))
        if do_norm:
            act(RSTD.ap, P(b), AF.Sqrt, PS.k(b) + CEPS.k(), RSTD.k(), scale=1.0 / D, bias=CEPS.ap)
            sch.op("dve", lambda e: e.reciprocal(out=RSTD.ap, in_=RSTD.ap), RSTD.k(), RSTD.k())
        for fc in (range(16) if do_norm else ()):
            stt(XT.ap[:, fc, :], XT.ap[:, fc, :], FNW.ap[:, fc:fc + 1], RSTD.ap, ALU.mult, ALU.mult, XT.k() + RSTD.k() + FNW.k(), XT.k())
        for c in range(4):
            for f0 in range(0, 16, 4):
                b = ps_rot()
                for j in range(4):
                    tr(P(b)[:, j * 128:(j + 1) * 128], XT.ap[:, f0 + j, c * 128:(c + 1) * 128], IDF.ap, XT.k() + IDF.k(), PS.k(b))
                cp(STGo.ap[:, f0 * 128:(f0 + 4) * 128], P(b), PS.k(b), STGo.k())
            sch.dma("sp", out_d[t0 + c * 128:t0 + (c + 1) * 128, :], STGo.ap, reads=STGo.k())


    for l in range(L):
        vk = VEC.k()
        vec_cols(VEC.ap[:, V_N1:V_N1 + 16], W['norm1_w'][l].rearrange("(a p) -> a p", p=128), 16, vk)
        vec_cols(VEC.ap[:, V_N2:V_N2 + 16], W['norm2_w'][l].rearrange("(a p) -> a p", p=128), 16, vk)
        vec_cols(VEC.ap[:, V_ADAB:V_ADAB + 96], W['ada_b'][l].rearrange("(a p) -> a p", p=128), 96, vk)
        vec_cols(VEC.ap[:, V_SCW:V_SCW + 64], W['ssd_conv_w'][l].rearrange("k (a p) -> (k a) p", p=128), 64, vk)
        vec_cols(VEC.ap[:, V_SCB:V_SCB + 16], W['ssd_conv_b'][l].rearrange("(a p) -> a p", p=128), 16, vk)
        vec_cols(VEC.ap[:, V_LCW:V_LCW + 32], W['lru_conv_w'][l].rearrange("k (a p) -> (k a) p", p=128), 32, vk)
        vec_cols(VEC.ap[:, V_LCB:V_LCB + 8], W['lru_conv_b'][l].rearrange("(a p) -> a p", p=128), 8, vk)
        vec_cols(VEC.ap[:, V_LBA:V_LBA + 8], W['lru_ba'][l].rearrange("(a p) -> a p", p=128), 8, vk)
        vec_cols(VEC.ap[:, V_LBX:V_LBX + 8], W['lru_bx'][l].rearrange("(a p) -> a p", p=128), 8, vk)
        vec_cols(VEC.ap[:, V_LLAM:V_LLAM + 8], W['lru_lambda'][l].rearrange("(a p) -> a p", p=128), 8, vk)
        fcw = W['ffn_conv_w'][l].rearrange("k (a p) -> k a p", p=128)
        for k in range(3):
            vec_cols(VEC.ap[:, V_FCW + 88 * k:V_FCW + 88 * (k + 1)], fcw[k], 88, vk)
        vec_cols(VEC.ap[:, V_FCB:V_FCB + 88], W['ffn_conv_b'][l].rearrange("(a p) -> a p", p=128), 88, vk)
        vec_cols(VEC.ap[:, V_S5D:V_S5D + 6], W['s5_d'][l].rearrange("(a p) -> a p", p=128), 6, vk)
        vec_cols(VEC.ap[:, V_SNW:V_SNW + 8], W['ssd_norm_w'][l].rearrange("(a p) -> a p", p=128), 8, vk)
        vec_cols(VEC.ap[:, V_GNW:V_GNW + 8], W['ret_gn_w'][l].rearrange("(a p) -> a p", p=128), 8, vk)
        sch.dma("sp", ROWS.ap[:, 0, :], W['ssd_dt_bias'][l].partition_broadcast(128), writes=ROWS.k())
        sch.dma("sp", ROWS.ap[:, 1, :], W['ssd_a_log'][l].partition_broadcast(128), writes=ROWS.k())
        sch.dma("sp", ROWS.ap[:, 2, :], W['ssd_d'][l].partition_broadcast(128), writes=ROWS.k())
        act(ROWS.ap[:, 1, :], ROWS.ap[:, 1, :], AF.Exp, ROWS.k(), ROWS.k())
        ts(ROWS.ap[:, 1, :], ROWS.ap[:, 1, :], -1.0, None, ALU.mult, None, ROWS.k(), ROWS.k())
        act(C8.ap, VEC.ap[:, V_LLAM:V_LLAM + 8], AF.Exp, vk, C8.k(), scale=-1.0)
        act(C8.ap, C8.ap, AF.Ln, C8.k() + CONE.k(), C8.k(), bias=CONE.ap)
        ts(C8.ap, C8.ap, -8.0, None, ALU.mult, None, C8.k(), C8.k())
        for tl in (CAR, HL, HALS, HALL, FH, HS, RS):
            memset(tl.ap, 0.0, tl.k())
        mark('vecs')
        bm = ps_rot()
        for j0 in range(0, 96, 2):
            wsc[0] = (wsc[0] + 1) % 2
            s = WS[wsc[0]]
            v = s.ap.bitcast(F32).rearrange("p (k c) -> p k c", k=16)
            sch.dma("sp", v, W['ada_w'][l][:, j0 * 128:(j0 + 2) * 128].rearrange("(k p) c -> p k c", p=128), writes=s.k())
            for j in range(2):
                for kc in range(16):
                    mm(P(bm)[:, j0 + j:j0 + j + 1], v[:, kc, j * 128:(j + 1) * 128], CT.ap[:, kc:kc + 1], kc == 0, kc == 15,
                       s.k() + CT.k(), PS.k(bm))
        tt(MOD.ap, P(bm)[:, 0:96], VEC.ap[:, V_ADAB:V_ADAB + 96], ALU.add, PS.k(bm) + vk, MOD.k())
        stt(WE.ap[:, 0, :], MOD.ap[:, 16:32], 1.0, VEC.ap[:, V_N1:V_N1 + 16], ALU.add, ALU.mult, MOD.k() + vk, WE.k())
        stt(WE.ap[:, 1, :], MOD.ap[:, 64:80], 1.0, VEC.ap[:, V_N2:V_N2 + 16], ALU.add, ALU.mult, MOD.k() + vk, WE.k())
        dbgout("mod%d" % l, MOD.ap, MOD.k())

        mark('mod')
        sch.dma("sp", MASKB.ap, CD['maskb'].rearrange("p (j g c) -> p j g c", j=4, g=8), writes=MASKB.k())
        sch.dma("sp", MASKC.ap, CD['maskc'].rearrange("p (j c) -> p j c", j=4), writes=MASKC.k())
        o = 20480
        LAM = View(AR, o, 1024, F32, "p (a b) -> p a b", a=2); o += 1024
        LDT = View(AR, o, 1024, F32); o += 1024
        sv = {}
        for nm in ("lr", "lrd", "ang", "mag", "sin", "cos", "ar", "ai", "den", "am1", "cr", "ci", "t1", "t2", "tA", "tB"):
            sv[nm] = View(AR, o, 1024, F32); o += 1024
        svI = View(AR, o, 1024, I32); o += 1024
        sch.dma("sp", LAM.ap[:24, 0, :], W['s5_lam_re'][l].rearrange("(k h) p -> k (h p)", h=2), writes=LAM.k())
        sch.dma("sp", LAM.ap[:24, 1, :], W['s5_lam_im'][l].rearrange("(k h) p -> k (h p)", h=2), writes=LAM.k())
        sch.dma("sp", LDT.ap[:24, 0:2], W['s5_log_dt'][l].rearrange("(k h) -> k h", h=2), writes=LDT.k())
        act(LDT.ap[:24, 0:2], LDT.ap[:24, 0:2], AF.Exp, LDT.k(), LDT.k())
        A_ = lambda nm: sv[nm].ap[:24, 0:128]
        K_ = lambda *nms: sum([sv[n].k() for n in nms], [])
        ts(A_("lr"), LAM.ap[:24, 0, :], -1e-4, None, ALU.min, None, LAM.k(), K_("lr"))
        for h in range(2):
            hs = slice(h * 64, (h + 1) * 64)
            ts(A_("lrd")[:, hs], A_("lr")[:, hs], LDT.ap[:24, h:h + 1], None, ALU.mult, None, K_("lr") + LDT.k(), K_("lrd"))
            ts(A_("ang")[:, hs], LAM.ap[:24, 1, hs], LDT.ap[:24, h:h + 1], None, ALU.mult, None, LAM.k() + LDT.k(), K_("ang"))
        act(A_("mag"), A_("lrd"), AF.Exp, K_("lrd"), K_("mag"))
        sincos(A_("ang"), 128, A_("sin"), A_("cos"), A_("tA"), svI.ap[:24, 0:128], A_("tB"), K_("tA"), svI.k(), K_("tB"),
               K_("ang"), K_("sin"), K_("cos"))
        tt(A_("ar"), A_("mag"), A_("cos"), ALU.mult, K_("mag", "cos"), K_("ar"))
        tt(A_("ai"), A_("mag"), A_("sin"), ALU.mult, K_("mag", "sin"), K_("ai"))
        tt(A_("den"), A_("lr"), A_("lr"), ALU.mult, K_("lr"), K_("den"))
        tt(A_("t1"), LAM.ap[:24, 1, :], LAM.ap[:24, 1, :], ALU.mult, LAM.k(), K_("t1"))
        tt(A_("den"), A_("den"), A_("t1"), ALU.add, K_("den", "t1"), K_("den"))
        sch.op("dve", lambda e: e.reciprocal(out=A_("den"), in_=A_("den")), K_("den"), K_("den"))
        ts(A_("am1"), A_("ar"), -1.0, None, ALU.add, None, K_("ar"), K_("am1"))
        tt(A_("t1"), A_("am1"), A_("lr"), ALU.mult, K_("am1", "lr"), K_("t1"))
        tt(A_("t2"), A_("ai"), LAM.ap[:24, 1, :], ALU.mult, K_("ai") + LAM.k(), K_("t2"))
        tt(A_("t1"), A_("t1"), A_("t2"), ALU.add, K_("t1", "t2"), K_("t1"))
        tt(A_("cr"), A_("t1"), A_("den"), ALU.mult, K_("t1", "den"), K_("cr"))
        tt(A_("t1"), A_("ai"), A_("lr"), ALU.mult, K_("ai", "lr"), K_("t1"))
        tt(A_("t2"), A_("am1"), LAM.ap[:24, 1, :], ALU.mult, K_("am1") + LAM.k(), K_("t2"))
        tt(A_("t1"), A_("t1"), A_("t2"), ALU.subtract, K_("t1", "t2"), K_("t1"))
        tt(A_("ci"), A_("t1"), A_("den"), ALU.mult, K_("t1", "den"), K_("ci"))
        for i, nm in enumerate(("ar", "ai", "cr", "ci")):
            b = ps_rot()
            tr(P(b)[:, 0:24], A_(nm), IDF.ap[:24, :24], K_(nm) + IDF.k(), PS.k(b))
            cp(S5S.ap[:, i, :], P(b)[:, 0:24], PS.k(b), S5S.k())
        cp(PW.ap[:, 0, :, 0], S5S.ap[:, 0, :], S5S.k(), PW.k())
        cp(PW.ap[:, 1, :, 0], S5S.ap[:, 1, :], S5S.k(), PW.k())
        T1 = View(AR, o, 512, F32); o += 1024
        T2 = View(AR, o, 512, F32); o += 1024
        for j in range(1, 9):
            pr, pi_ = PW.ap[:, 0, :, j - 1], PW.ap[:, 1, :, j - 1]
            tt(T1.ap[:, 0:24], pr, pr, ALU.mult, PW.k(), T1.k())
            tt(T2.ap[:, 0:24], pi_, pi_, ALU.mult, PW.k(), T2.k())
            tt(PW.ap[:, 0, :, j], T1.ap[:, 0:24], T2.ap[:, 0:24], ALU.subtract, T1.k() + T2.k(), PW.k())
            stt(PW.ap[:, 1, :, j], pr, 2.0, pi_, ALU.mult, ALU.mult, PW.k(), PW.k())
        ts(PW.ap[:, 2, :, :], PW.ap[:, 1, :, :], -1.0, None, ALU.mult, None, PW.k(), PW.k())
        o = 45056
        BR = View(AR, o, 1536, F32, "p (k c) -> p k c", k=24); o += 2048
        BI = View(AR, o, 1536, F32, "p (k c) -> p k c", k=24); o += 2048
        BBR = View(AR, o, 1536, F32, "p (k c) -> p k c", k=24); o += 2048
        BBI = View(AR, o, 1536, F32, "p (k c) -> p k c", k=24); o += 2048
        BT = View(AR, o, 1536, F32, "p (k c) -> p k c", k=24); o += 2048
        SRC4 = View(AR, o, 2048, F32, "p (j q) -> p j q", j=4); o += 2048
        WB4 = View(AR, o, 1024, BF16, "p (j q) -> p j q", j=4); o += 1024
        CC = View(AR, o, 6144, F32, "p (a r q) -> p a r q", a=6, r=2); o += 6144
        for h in range(2):
            sch.dma("sp", BR.ap[h * 64:(h + 1) * 64], W['s5_b_re'][l].rearrange("(k h) p c -> h p k c", h=2)[h], writes=BR.k())
            sch.dma("sp", BI.ap[h * 64:(h + 1) * 64], W['s5_b_im'][l].rearrange("(k h) p c -> h p k c", h=2)[h], writes=BI.k())
        crb = S5S.ap[:, 2, :].unsqueeze(2).broadcast_to([128, 24, 16])
        cib = S5S.ap[:, 3, :].unsqueeze(2).broadcast_to([128, 24, 16])
        tt(BBR.ap, BR.ap, crb, ALU.mult, BR.k() + S5S.k(), BBR.k())
        tt(BT.ap, BI.ap, cib, ALU.mult, BI.k() + S5S.k(), BT.k())
        tt(BBR.ap, BBR.ap, BT.ap, ALU.subtract, BBR.k() + BT.k(), BBR.k())
        tt(BBI.ap, BI.ap, crb, ALU.mult, BI.k() + S5S.k(), BBI.k())
        tt(BT.ap, BR.ap, cib, ALU.mult, BR.k() + S5S.k(), BT.k())
        tt(BBI.ap, BBI.ap, BT.ap, ALU.add, BBI.k() + BT.k(), BBI.k())
        for a in range(6):
            for ri, BBx in enumerate((BBR, BBI)):
                tt(SRC4.ap.rearrange("p j (g c) -> p j g c", g=8),
                   BBx.ap[:, 4 * a:4 * a + 4, :].unsqueeze(2).broadcast_to([128, 4, 8, 16]), MASKB.ap, ALU.mult,
                   BBx.k() + MASKB.k(), SRC4.k())
                b = ps_rot()
                for j in range(4):
                    tr(P(b)[:, j * 128:(j + 1) * 128], SRC4.ap[:, j, :], IDF.ap, SRC4.k() + IDF.k(), PS.k(b))
                cp(WB4.ap, P(b).rearrange("p (j q) -> p j q", j=4), PS.k(b), WB4.k())
                sch.dma("sp", S5WB[l, :, 4 * a:4 * a + 4, ri, :], WB4.ap, reads=WB4.k(), writes=[("s5w", l)])
        for ri, nm in enumerate(('s5_c_re', 's5_c_im')):
            src = W[nm][l].rearrange("(a g) c p -> (g c) a p", a=6)
            sch.dma("sp", CC.ap[:, :, ri, 0:64], src, writes=CC.k())
            sch.dma("sp", CC.ap[:, :, ri, 64:128], src, writes=CC.k())
        for a in range(6):
            for ri in range(2):
                b = ps_rot()
                tr(P(b)[:, 0:128], CC.ap[:, a, ri, :], IDF.ap, CC.k() + IDF.k(), PS.k(b))
                stt(WB4.ap, P(b)[:, 0:128].unsqueeze(1).broadcast_to([128, 4, 128]), (1.0 if ri == 0 else -1.0), MASKC.ap,
                    ALU.mult, ALU.mult, PS.k(b) + MASKC.k(), WB4.k())
                sch.dma("sp", S5WC[l, :, 4 * a:4 * a + 4, ri, :], WB4.ap, reads=WB4.k(), writes=[("s5w", l)])
        for m in range(6):
            ts(DIAGD.ap[:, m, :], IDF.ap, VEC.ap[:, V_S5D + m:V_S5D + m + 1], None, ALU.mult, None, IDF.k() + vk, DIAGD.k())

        mark('s5setup')
        for t in range(NTL):
            t0 = t * TT
            sch.dma("sp", XT.ap, XD[:, :, t0:t0 + TT].rearrange("f p t -> p f t"), reads=[("xd", t)], writes=XT.k())
            sch.dma("sp", ROPC.ap, ROPD[0, :, t0:t0 + TT], reads=[("ropd", t)], writes=ROPC.k())
            sch.dma("sp", ROPS.ap, ROPD[1, :, t0:t0 + TT], reads=[("ropd", t)], writes=ROPS.k())
            norm_mod(lambda fc: WE.ap[:, 0, fc:fc + 1], lambda fc: MOD.ap[:, fc:fc + 1], WE.k() + MOD.k())
            if l == 0 and t == 0:
                dbgout("h1", HT.ap[:, 0, :], HT.k())

            mark('norm1')
            U5 = View(AR, 0, 6144, BF16, "p (m t) -> p m t", m=6)
            WBt = View(AR, 6144, 2048, BF16, "p (k r q) -> p k r q", k=4, r=2)
            WCt = View(AR, 8192, 2048, BF16, "p (k r q) -> p k r q", k=4, r=2)
            XA = [View(AR, 10240 + i * 3072, 3072, F32) for i in range(4)]
            XZ = View(AR, 22528, 2048, BF16, "p (r t) -> p r t", r=2)
            Y5 = View(AR, 24576, 6144, BF16, "p (m t) -> p m t", m=6)
            SG = View(AR, 30720, 2048, F32)
            for xa in XA:
                memset(xa.ap[:, 0:256], 0.0, xa.k())

            def c_u(m, b):
                act(U5.ap[:, m, :], P(b), AF.Copy, PS.k(b), U5.k())
            proj_fm(l, O_U, 6, c_u)
            by = 4
            for k in range(24):
                m = k // 4
                if k % 4 == 0:
                    sch.dma("sp", WBt.ap, S5WB[l, :, k:k + 4, :, :], reads=[("s5w", l)], writes=WBt.k())
                    sch.dma("sp", WCt.ap, S5WC[l, :, k:k + 4, :, :], reads=[("s5w", l)], writes=WCt.k())
                for ri in range(2):
                    b = ps_rot()
                    mm(P(b), WBt.ap[:, k % 4, ri, :], U5.ap[:, m, :], True, True, WBt.k() + U5.k(), PS.k(b))
                    act(XA[ri].ap[:, 256:768], P(b), AF.Copy, PS.k(b), XA[ri].k())
                ar_, ai_, nai_ = PW.ap[:, 0, k, 0:1], PW.ap[:, 1, k, 0:1], PW.ap[:, 2, k, 0:1]
                cr_, ci_ = CAR.ap[:, 0, k:k + 1], CAR.ap[:, 1, k:k + 1]
                x0r, x0i = XA[0].ap[:, 256:257], XA[1].ap[:, 256:257]
                stt(x0r, cr_, ar_, x0r, ALU.mult, ALU.add, CAR.k() + PW.k() + XA[0].k(), XA[0].k())
                stt(x0r, ci_, nai_, x0r, ALU.mult, ALU.add, CAR.k() + PW.k() + XA[0].k(), XA[0].k())
                stt(x0i, ci_, ar_, x0i, ALU.mult, ALU.add, CAR.k() + PW.k() + XA[1].k(), XA[1].k())
                stt(x0i, cr_, ai_, x0i, ALU.mult, ALU.add, CAR.k() + PW.k() + XA[1].k(), XA[1].k())
                for j in range(9):
                    d = 1 << j
                    sr, si = (XA[0], XA[1]) if j % 2 == 0 else (XA[2], XA[3])
                    dr, di = (XA[2], XA[3]) if j % 2 == 0 else (XA[0], XA[1])
                    pr_, pi_, npi_ = PW.ap[:, 0, k, j:j + 1], PW.ap[:, 1, k, j:j + 1], PW.ap[:, 2, k, j:j + 1]
                    cur = slice(256, 768)
                    sh = slice(256 - d, 768 - d)
                    rk = sr.k() + si.k() + PW.k()
                    stt(dr.ap[:, cur], si.ap[:, sh], npi_, sr.ap[:, cur], ALU.mult, ALU.add, rk, dr.k())
                    stt(dr.ap[:, cur], sr.ap[:, sh], pr_, dr.ap[:, cur], ALU.mult, ALU.add, rk + dr.k(), dr.k())
                    stt(di.ap[:, cur], sr.ap[:, sh], pi_, si.ap[:, cur], ALU.mult, ALU.add, rk, di.k())
                    stt(di.ap[:, cur], si.ap[:, sh], pr_, di.ap[:, cur], ALU.mult, ALU.add, rk + di.k(), di.k())
                fr, fi = XA[2], XA[3]
                cp(CAR.ap[:, 0, k:k + 1], fr.ap[:, 767:768], fr.k(), CAR.k())
                cp(CAR.ap[:, 1, k:k + 1], fi.ap[:, 767:768], fi.k(), CAR.k())
                act(XZ.ap[:, 0, :], fr.ap[:, 256:768], AF.Copy, fr.k(), XZ.k())
                act(XZ.ap[:, 1, :], fi.ap[:, 256:768], AF.Copy, fi.k(), XZ.k())
                mm(P(by), WCt.ap[:, k % 4, 0, :], XZ.ap[:, 0, :], k % 4 == 0, False, WCt.k() + XZ.k(), PS.k(by))
                mm(P(by), WCt.ap[:, k % 4, 1, :], XZ.ap[:, 1, :], False, False, WCt.k() + XZ.k(), PS.k(by))
                if k % 4 == 3:
                    mm(P(by), DIAGD.ap[:, m, :], U5.ap[:, m, :], False, True, DIAGD.k() + U5.k(), PS.k(by))
                    act(Y5.ap[:, m, :], P(by), AF.Gelu_apprx_tanh, PS.k(by), Y5.k())
            s, v = load_slab(WB['s5_glu_w'][l], 6, 768, [("wb", 's5_glu_w', l)])
            for ct in range(6):
                b = ps_rot()
                for kc in range(6):
                    mm(P(b), v[:, kc, ct * 128:(ct + 1) * 128], Y5.ap[:, kc, :], kc == 0, kc == 5, s.k() + Y5.k(), PS.k(b))
                act(SG.ap, P(b), AF.Sigmoid, PS.k(b), SG.k())
                tt(YA.ap[:, ct, :], Y5.ap[:, ct, :], SG.ap, ALU.mult, Y5.k() + SG.k(), YA.k())
            if l == 0 and t == 0:
                dbgout("ya", YA.ap[:, 0, :], YA.k())

            mark('s5')
            o = 0
            GATE = View(AR, o, 8192, BF16, "p (m t) -> p m t", m=8); o += 8192
            CS = View(AR, o, 3072, F32); o += 3072
            XC = View(AR, o, 2048, F32); o += 2048
            XCB = View(AR, o, 1024, BF16); o += 1024
            WAX = View(AR, o, 4096, BF16, "p (w n d) -> p w n d", w=2, n=8); o += 4096
            Rr = View(AR, o, 2048, F32); o += 2048
            Ii = View(AR, o, 2048, F32); o += 2048
            Aa = View(AR, o, 2048, F32); o += 2048
            Sq_ = View(AR, o, 2048, F32); o += 2048
            Hh = View(AR, o, 2048, F32); o += 2048

            def c_xg(m, b):
                act(GATE.ap[:, m, :], P(b), AF.Gelu_apprx_tanh, PS.k(b), GATE.k())
            proj_fm(l, O_XG, 8, c_xg)
            sch.dma("sp", WAX.ap[:, 0], WB['lru_wa'][l].rearrange("n c d -> c n d"), reads=[("wb", 'lru_wa', l)], writes=WAX.k())
            sch.dma("sp", WAX.ap[:, 1], WB['lru_wx'][l].rearrange("n c d -> c n d"), reads=[("wb", 'lru_wx', l)], writes=WAX.k())

            def conv_fm(b, halo_ap, halo_k, K, wcol, bcol, out_ap, out_k):
                H = K - 1
                cp(CS.ap[:, 0:H], halo_ap, halo_k, CS.k())
                act(CS.ap[:, H:H + TT], P(b), AF.Copy, PS.k(b), CS.k())
                cp(halo_ap, CS.ap[:, TT:TT + H], CS.k(), halo_k)
                act(out_ap, CS.ap[:, 0:TT], AF.Identity, CS.k() + vk, out_k, scale=wcol(0), bias=bcol)
                for kk in range(1, K):
                    stt(out_ap, CS.ap[:, kk:kk + TT], wcol(kk), out_ap, ALU.mult, ALU.add, CS.k() + vk + out_k, out_k)

            def c_xr(m, b):
                conv_fm(b, HALL.ap[:, m, :], HALL.k(), 4, lambda kk: VEC.ap[:, V_LCW + kk * 8 + m:V_LCW + kk * 8 + m + 1],
                        VEC.ap[:, V_LCB + m:V_LCB + m + 1], XC.ap, XC.k())
                act(XCB.ap, XC.ap, AF.Copy, XC.k(), XCB.k())
                b1, b2 = ps_rot(), ps_rot()
                mm(P(b1), WAX.ap[:, 0, m, :], XCB.ap, True, True, WAX.k() + XCB.k(), PS.k(b1))
                mm(P(b2), WAX.ap[:, 1, m, :], XCB.ap, True, True, WAX.k() + XCB.k(), PS.k(b2))
                act(Rr.ap, P(b1), AF.Sigmoid, PS.k(b1) + vk, Rr.k(), bias=VEC.ap[:, V_LBA + m:V_LBA + m + 1])
                act(Ii.ap, P(b2), AF.Sigmoid, PS.k(b2) + vk, Ii.k(), bias=VEC.ap[:, V_LBX + m:V_LBX + m + 1])
                act(Aa.ap, Rr.ap, AF.Exp, Rr.k() + C8.k(), Aa.k(), scale=C8.ap[:, m:m + 1])
                tt(Sq_.ap, Aa.ap, Aa.ap, ALU.mult, Aa.k(), Sq_.k())
                act(Sq_.ap, Sq_.ap, AF.Sqrt, Sq_.k() + CONE.k(), Sq_.k(), scale=-1.0, bias=CONE.ap)
                tt(Ii.ap, Ii.ap, XC.ap, ALU.mult, Ii.k() + XC.k(), Ii.k())
                tt(Ii.ap, Ii.ap, Sq_.ap, ALU.mult, Ii.k() + Sq_.k(), Ii.k())
                sch.op("dve", lambda e: e.tensor_tensor_scan(out=Hh.ap, data0=Aa.ap, data1=Ii.ap, initial=HL.ap[:, m:m + 1],
                                                             op0=ALU.mult, op1=ALU.add), Aa.k() + Ii.k() + HL.k(), Hh.k())
                cp(HL.ap[:, m:m + 1], Hh.ap[:, TT - 1:TT], Hh.k(), HL.k())
                tt(YC.ap[:, m, :], Hh.ap, GATE.ap[:, m, :], ALU.mult, Hh.k() + GATE.k(), YC.k())
            proj_fm(l, O_XR, 8, c_xr)
            if l == 0 and t == 0:
                dbgout("yc", YC.ap[:, 0, :], YC.k())

            mark('lru')
            o = 0
            CS = View(AR, o, 3072, F32); o += 3072
            XC = View(AR, o, 2048, F32); o += 2048
            XBM = View(AR, o, 1024, BF16); o += 1024
            XSTM = View(AR, o, 8192, BF16, "p (c f) -> p c f", c=4); o += 8192
            BF_ = View(AR, o, 4096, BF16, "p (g t) -> p g t", g=4); o += 4096
            BTM = View(AR, o, 4096, BF16, "p (c f) -> p c f", c=4); o += 4096
            CF_ = View(AR, o, 4096, BF16, "p (g t) -> p g t", g=4); o += 4096
            ZS = View(AR, o, 8192, BF16, "p (c f) -> p c f", c=4); o += 8192
            DTT = View(AR, o, 512, F32, "p (a c h) -> p a c h", a=2, c=4); o += 1024
            RHSB = View(AR, o, 8192, F32, "p (h l) -> p h l", h=16); o += 8192
            SEG = View(AR, o, 8192, F32, "p (h l) -> p h l", h=16); o += 8192
            MT = View(AR, o, 4096, BF16, "p (h l) -> p h l", h=16); o += 4096
            CSC = View(AR, o, 4096, BF16, "p (h l) -> p h l", h=16); o += 4096
            XDT = View(AR, o, 2048, BF16); o += 2048
            XDD = View(AR, o, 2048, BF16); o += 2048
            HTB = View(AR, o, 2048, BF16); o += 2048
            Y1 = View(AR, 0, 4096, F32)
            YBT = View(AR, 4096, 2048, BF16)
            ACOL = View(AR, o, 1024, F32); o += 1024
            assert o <= 67584, o

            def c_xbc(m, b):
                conv_fm(b, HALS.ap[:, m, :], HALS.k(), 4, lambda kk: VEC.ap[:, V_SCW + kk * 16 + m:V_SCW + kk * 16 + m + 1],
                        VEC.ap[:, V_SCB + m:V_SCB + m + 1], XC.ap, XC.k())
                if m < 8 or 8 <= m < 12:
                    dst = XBM.ap if m < 8 else BF_.ap[:, m - 8, :]
                    dk = XBM.k() if m < 8 else BF_.k()
                    act(dst, XC.ap, AF.Silu, XC.k(), dk)
                    b2 = ps_rot()
                    pb = P(b2).bitcast(BF16)
                    for c in range(4):
                        tr(pb[:, c * 128:(c + 1) * 128], dst[:, c * 128:(c + 1) * 128], IDB.ap, dk + IDB.k(), PS.k(b2))
                    if m < 8:
                        cp(XSTM.ap[:, :, m * 128:(m + 1) * 128], pb[:, 0:512].rearrange("p (c i) -> p c i", c=4), PS.k(b2), XSTM.k())
                    else:
                        cp(BTM.ap[:, :, (m - 8) * 128:(m - 7) * 128], pb[:, 0:512].rearrange("p (c i) -> p c i", c=4), PS.k(b2), BTM.k())
                else:
                    act(CF_.ap[:, m - 12, :], XC.ap, AF.Silu, XC.k(), CF_.k())
            proj_fm(l, O_XBC, 16, c_xbc)
            for sidx in range(2):
                def c_z(c, b, sidx=sidx):
                    act(ZS.ap[:, c, sidx * 512:(sidx + 1) * 512], P(b), AF.Silu, PS.k(b), ZS.k())
                proj_tm(l, O_Z + sidx * 512, 512, c_z)

            def c_dt(c, b):
                tt(DTT.ap[:, 0, c, :], P(b)[:, 0:16], ROWS.ap[:, 0, :], ALU.add, PS.k(b) + ROWS.k(), DTT.k())
                act(DTT.ap[:, 0, c, :], DTT.ap[:, 0, c, :], AF.Exp, DTT.k(), DTT.k())
                act(DTT.ap[:, 0, c, :], DTT.ap[:, 0, c, :], AF.Ln, DTT.k() + CONE.k(), DTT.k(), bias=CONE.ap)
                tt(DTT.ap[:, 1, c, :], DTT.ap[:, 0, c, :], ROWS.ap[:, 1, :], ALU.mult, DTT.k() + ROWS.k(), DTT.k())
            proj_tm(l, O_DT, 16, c_dt)
            for c in range(4):
                cs = slice(c * 128, (c + 1) * 128)
                dta = DTT.ap[:, 1, c, :]
                b = ps_rot()
                mm(P(b)[:, 0:16], TRI.ap, dta, True, True, TRI.k() + DTT.k(), PS.k(b))
                ts(ACOL.ap[:, 0:16], P(b)[:, 0:16], -1.0, None, ALU.mult, None, PS.k(b), ACOL.k())
                tt(RHSB.ap, dta.unsqueeze(2).broadcast_to([128, 16, 128]), TRI.ap.unsqueeze(1).broadcast_to([128, 16, 128]), ALU.mult,
                   DTT.k() + TRI.k(), RHSB.k())
                for j in range(4):
                    mm(P(4 + j), ONF.ap, RHSB.ap[:, 4 * j:4 * j + 4, :], True, True, ONF.k() + RHSB.k(), PS.k(4 + j))
                pbig = PS.ap[:, 4:8, :].rearrange("p j (h l) -> p (j h) l", h=4)
                tt(SEG.ap, pbig, NEGM.ap.unsqueeze(1).broadcast_to([128, 16, 128]), ALU.add, PS.k(range(4, 8)) + NEGM.k(), SEG.k())
                tt(SEG.ap, SEG.ap, ACOL.ap[:, 0:16].unsqueeze(2).broadcast_to([128, 16, 128]), ALU.add, SEG.k() + ACOL.k(), SEG.k())
                act(SEG.ap, SEG.ap, AF.Exp, SEG.k(), SEG.k())
                act(RHSB.ap, pbig, AF.Exp, PS.k(range(4, 8)), RHSB.k())
                b = ps_rot()
                for g in range(4):
                    mm(P(b)[:, g * 128:(g + 1) * 128], BF_.ap[:, g, cs], CF_.ap[:, g, cs], True, True, BF_.k() + CF_.k(), PS.k(b))
                tt(MT.ap.rearrange("p (g r) l -> p g r l", g=4), SEG.ap.rearrange("p (g r) l -> p g r l", g=4),
                   P(b).rearrange("p (g l) -> p g l", g=4).unsqueeze(2).broadcast_to([128, 4, 4, 128]), ALU.mult,
                   SEG.k() + PS.k(b), MT.k())
                tt(CSC.ap.rearrange("p (g r) l -> p g r l", g=4), RHSB.ap.rearrange("p (g r) l -> p g r l", g=4),
                   CF_.ap[:, :, cs].unsqueeze(2).broadcast_to([128, 4, 4, 128]), ALU.mult, RHSB.k() + CF_.k(), CSC.k())
                tt(XDT.ap.rearrange("p (h d) -> p h d", h=16), XSTM.ap[:, c, :].rearrange("p (h d) -> p h d", h=16),
                   DTT.ap[:, 0, c, :].unsqueeze(2).broadcast_to([128, 16, 64]), ALU.mult, XSTM.k() + DTT.k(), XDT.k())
                tt(XDD.ap.rearrange("p (h d) -> p h d", h=16), XDT.ap.rearrange("p (h d) -> p h d", h=16),
                   SEG.ap[:, :, 127:128].broadcast_to([128, 16, 64]), ALU.mult, XDT.k() + SEG.k(), XDD.k())
                act(HTB.ap, HS.ap, AF.Copy, HS.k(), HTB.k())
                for h in range(16):
                    bb = 4 + h // 8
                    oo = P(bb)[:, (h % 8) * 64:(h % 8 + 1) * 64]
                    mm(oo, MT.ap[:, h, :], XDT.ap[:, h * 64:(h + 1) * 64], True, False, MT.k() + XDT.k(), PS.k(bb))
                    mm(oo, CSC.ap[:, h, :], HTB.ap[:, h * 64:(h + 1) * 64], False, True, CSC.k() + HTB.k(), PS.k(bb))
                py = PS.ap[:, 4:6, :].rearrange("p j f -> p (j f)")
                tt(Y1.ap.rearrange("p (h d) -> p h d", h=16), XSTM.ap[:, c, :].rearrange("p (h d) -> p h d", h=16),
                   ROWS.ap[:, 2, :].unsqueeze(2).broadcast_to([128, 16, 64]), ALU.mult, XSTM.k() + ROWS.k(), Y1.k())
                tt(Y1.ap, Y1.ap, py, ALU.add, Y1.k() + PS.k([4, 5]), Y1.k())
                tt(Y1.ap, Y1.ap, ZS.ap[:, c, :], ALU.mult, Y1.k() + ZS.k(), Y1.k())
                for g in range(4):
                    bb = 6 + g // 2
                    mm(P(bb)[:, (g % 2) * 256:(g % 2 + 1) * 256], BTM.ap[:, c, g * 128:(g + 1) * 128], XDD.ap[:, g * 256:(g + 1) * 256],
                       True, True, BTM.k() + XDD.k(), PS.k(bb))
                tt(HS.ap.rearrange("p (h d) -> p h d", h=16), HS.ap.rearrange("p (h d) -> p h d", h=16),
                   RHSB.ap[:, :, 127:128].broadcast_to([128, 16, 64]), ALU.mult, HS.k() + RHSB.k() + HTB.k(), HS.k())
                tt(HS.ap, HS.ap, PS.ap[:, 6:8, :].rearrange("p j f -> p (j f)"), ALU.add, HS.k() + PS.k([6, 7]), HS.k())
                act(SEG.ap.rearrange("p h l -> p (h l)")[:, 0:1024], Y1.ap, AF.Square, Y1.k(), SEG.k(), accum_out=ACOL.ap[:, 16:17])
                act(ACOL.ap[:, 17:18], ACOL.ap[:, 16:17], AF.Sqrt, SEG.k() + ACOL.k() + CEPS.k(), ACOL.k(), scale=1.0 / 1024, bias=CEPS.ap)
                sch.op("dve", lambda e, A=ACOL: e.reciprocal(out=A.ap[:, 17:18], in_=A.ap[:, 17:18]), ACOL.k(), ACOL.k())
                ts(YBT.ap, Y1.ap, ACOL.ap[:, 17:18], None, ALU.mult, None, Y1.k() + ACOL.k(), YBT.k())
                for m0 in range(0, 8, 4):
                    b = ps_rot()
                    pb = P(b).bitcast(BF16)
                    for j in range(4):
                        tr(pb[:, j * 128:(j + 1) * 128], YBT.ap[:, (m0 + j) * 128:(m0 + j + 1) * 128], IDB.ap, YBT.k() + IDB.k(), PS.k(b))
                    for j in range(4):
                        act(YB.ap[:, m0 + j, cs], pb[:, j * 128:(j + 1) * 128], AF.Copy, PS.k(b) + vk, YB.k(),
                            scale=VEC.ap[:, V_SNW + m0 + j:V_SNW + m0 + j + 1])
            if l == 0 and t == 0:
                dbgout("yb", YB.ap[:, 0, :], YB.k())

            mark('ssd')
            o = 0
            QB = View(AR, o, 1024, BF16); o += 1024
            RT1 = View(AR, o, 2048, F32); o += 2048
            RT2 = View(AR, o, 2048, F32); o += 2048
            QF = View(AR, o, 4096, BF16, "p (m t) -> p m t", m=4); o += 4096
            QX = View(AR, o, 8192, BF16, "p (m t) -> p m t", m=8); o += 8192
            KF = View(AR, o, 4096, BF16, "p (m t) -> p m t", m=4); o += 4096
            KM = View(AR, o, 8192, BF16, "p (m t) -> p m t", m=8); o += 8192
            KZ = View(AR, o, 4096, BF16, "p (c f) -> p c f", c=4); o += 4096
            VTM = View(AR, o, 8192, BF16, "p (c f) -> p c f", c=4); o += 8192
            GTM = View(AR, o, 8192, BF16, "p (c f) -> p c f", c=4); o += 8192
            ST = View(AR, o, 2048, BF16, "p (h i) -> p h i", h=8); o += 2048
            RB = View(AR, o, 1024, BF16, "p (t e) -> p t e", t=4); o += 1024
            OC = View(AR, o, 4096, F32, "p (h e) -> p h e", h=8); o += 4096
            OSQ = View(AR, o, 4096, F32, "p (h e) -> p h e", h=8); o += 4096
            YDT = View(AR, o, 2048, BF16); o += 2048
            STAT = View(AR, o, 1024, F32); o += 1024

            def c_qk(which):
                def f(m, b):
                    act(QB.ap, P(b), AF.Copy, PS.k(b), QB.k(), scale=(1.0 if which == 0 else 0.125))
                    b2 = ps_rot()
                    mm(P(b2), PERM.ap, QB.ap, True, True, PERM.k() + QB.k(), PS.k(b2))
                    tt(RT1.ap, QB.ap, ROPC.ap, ALU.mult, QB.k() + ROPC.k(), RT1.k())
                    tt(RT2.ap, P(b2), ROPS.ap, ALU.mult, PS.k(b2) + ROPS.k(), RT2.k())
                    tt(RT1.ap, RT1.ap, RT2.ap, ALU.add, RT1.k() + RT2.k(), RT1.k())
                    if which == 0:
                        act(QF.ap[:, m, :], RT1.ap, AF.Copy, RT1.k(), QF.k())
                        tt(RT2.ap.rearrange("p (c i) -> p c i", c=4), RT1.ap.rearrange("p (c i) -> p c i", c=4),
                           XIT.ap[:, m, :].unsqueeze(1).broadcast_to([128, 4, 128]), ALU.mult, RT1.k() + XIT.k(), RT2.k())
                        for hh in range(2):
                            ts(QX.ap[:, 2 * m + hh, :], RT2.ap, HM.ap[:, hh:hh + 1], None, ALU.mult, None, RT2.k() + HM.k(), QX.k())
                    else:
                        act(KF.ap[:, m, :], RT1.ap, AF.Copy, RT1.k(), KF.k())
                        for hh in range(2):
                            ts(KM.ap[:, 2 * m + hh, :], RT1.ap, HM.ap[:, hh:hh + 1], None, ALU.mult, None, RT1.k() + HM.k(), KM.k())
                        b3 = ps_rot()
                        pb = P(b3).bitcast(BF16)
                        for c in range(4):
                            tr(pb[:, c * 128:(c + 1) * 128], KF.ap[:, m, c * 128:(c + 1) * 128], IDB.ap, KF.k() + IDB.k(), PS.k(b3))
                        for hh in range(2):
                            ts(KZ.ap[:, :, (2 * m + hh) * 64:(2 * m + hh + 1) * 64],
                               pb[:, 0:512].rearrange("p (c i) -> p c i", c=4)[:, :, hh * 64:(hh + 1) * 64],
                               ZETA.ap[:, 2 * m + hh:2 * m + hh + 1], None, ALU.mult, None, PS.k(b3) + ZETA.k(), KZ.k())
                return f
            proj_fm(l, O_Q, 4, c_qk(0))
            proj_fm(l, O_K, 4, c_qk(1))
            for sidx in range(2):
                def c_v(c, b, sidx=sidx):
                    act(VTM.ap[:, c, sidx * 512:(sidx + 1) * 512], P(b), AF.Copy, PS.k(b), VTM.k())
                proj_tm(l, O_V + sidx * 512, 512, c_v)
            for sidx in range(2):
                def c_g(c, b, sidx=sidx):
                    act(GTM.ap[:, c, sidx * 512:(sidx + 1) * 512], P(b), AF.Silu, PS.k(b), GTM.k())
                proj_tm(l, O_G + sidx * 512, 512, c_g)
            for c in range(4):
                cs = slice(c * 128, (c + 1) * 128)
                for h in range(8):
                    tq, r0 = h // 2, (h % 2) * 64
                    bb = 4 + h // 4
                    mm(P(bb)[:, (h % 4) * 128:(h % 4 + 1) * 128], KM.ap[:, h, cs], QF.ap[:, tq, cs], True, True,
                       KM.k() + QF.k(), PS.k(bb))
                tt(ST.ap, PS.ap[:, 4:6, :].rearrange("p j (h i) -> p (j h) i", h=4), INTRA.ap, ALU.mult, PS.k([4, 5]) + INTRA.k(), ST.k())
                act(RB.ap, RS.ap, AF.Copy, RS.k(), RB.k())
                for h in range(8):
                    tq, r0 = h // 2, (h % 2) * 64
                    bb = 6 + h // 4
                    oo = P(bb)[:, (h % 4) * 128:(h % 4 + 1) * 128]
                    mm(oo, ST.ap[:, h, :], VTM.ap[:, c, h * 128:(h + 1) * 128], True, False, ST.k() + VTM.k(), PS.k(bb))
                    mm(oo, QX.ap[:, h, cs], RB.ap[:, tq, :], False, True, QX.k() + RB.k(), PS.k(bb))
                po = PS.ap[:, 6:8, :].rearrange("p j (h e) -> p (j h) e", h=4)
                bkv = [ps_rot(), ps_rot()]
                for tq in range(4):
                    b = bkv[tq // 2]
                    mm(P(b)[:, (tq % 2) * 256:(tq % 2 + 1) * 256], KZ.ap[:, c, tq * 128:(tq + 1) * 128], VTM.ap[:, c, tq * 256:(tq + 1) * 256],
                       True, True, KZ.k() + VTM.k(), PS.k(b))
                for tq in range(4):
                    b = bkv[tq // 2]
                    for hh in range(2):
                        rs = slice(hh * 64, (hh + 1) * 64)
                        c0_ = (tq % 2) * 256 + hh * 128
                        stt(RS.ap[rs, tq, :], RS.ap[rs, tq, :], GDEC.ap[rs, tq:tq + 1], P(b)[rs, c0_:c0_ + 128], ALU.mult, ALU.add,
                            RS.k() + GDEC.k() + PS.k(b) + RB.k(), RS.k())
                sch.op("dve", lambda e, po=po, S_=STAT: e.tensor_reduce(out=S_.ap[:, 0:8], in_=po, axis=AX.X, op=ALU.add), PS.k([6, 7]), STAT.k())
                ts(STAT.ap[:, 0:8], STAT.ap[:, 0:8], -1.0 / 128, None, ALU.mult, None, STAT.k(), STAT.k())
                tt(OC.ap, po, STAT.ap[:, 0:8].unsqueeze(2).broadcast_to([128, 8, 128]), ALU.add, PS.k([6, 7]) + STAT.k(), OC.k())
                tt(OSQ.ap, OC.ap, OC.ap, ALU.mult, OC.k(), OSQ.k())
                sch.op("dve", lambda e, S_=STAT, Q_=OSQ: e.tensor_reduce(out=S_.ap[:, 8:16], in_=Q_.ap, axis=AX.X, op=ALU.add), OSQ.k(), STAT.k())
                act(STAT.ap[:, 8:16], STAT.ap[:, 8:16], AF.Sqrt, STAT.k() + CEPS.k(), STAT.k(), scale=1.0 / 128, bias=CEPS.ap)
                sch.op("dve", lambda e, S_=STAT: e.reciprocal(out=S_.ap[:, 8:16], in_=S_.ap[:, 8:16]), STAT.k(), STAT.k())
                tt(OC.ap, OC.ap, STAT.ap[:, 8:16].unsqueeze(2).broadcast_to([128, 8, 128]), ALU.mult, OC.k() + STAT.k(), OC.k())
                tt(YDT.ap, OC.ap.rearrange("p h e -> p (h e)"), GTM.ap[:, c, :], ALU.mult, OC.k() + GTM.k(), YDT.k())
                for m0 in range(0, 8, 4):
                    b = ps_rot()
                    pb = P(b).bitcast(BF16)
                    for j in range(4):
                        tr(pb[:, j * 128:(j + 1) * 128], YDT.ap[:, (m0 + j) * 128:(m0 + j + 1) * 128], IDB.ap, YDT.k() + IDB.k(), PS.k(b))
                    for j in range(4):
                        act(YD.ap[:, m0 + j, cs], pb[:, j * 128:(j + 1) * 128], AF.Copy, PS.k(b) + vk, YD.k(),
                            scale=VEC.ap[:, V_GNW + m0 + j:V_GNW + m0 + j + 1])
            if l == 0 and t == 0:
                dbgout("yd", YD.ap[:, 0, :], YD.k())

            mark('ret')
            MERG = View(AR, 0, 16384, BF16, "p (m t) -> p m t", m=16)
            MG = View(AR, 16384, 8192, F32, "p (m t) -> p m t", m=4)
            SGm = View(AR, 24576, 2048, F32)
            TMm = View(AR, 26624, 2048, F32)
            branches = (('w_br_a', YA, 6), ('w_br_b', YB, 8), ('w_br_c', YC, 8), ('w_br_d', YD, 8))
            for m0 in range(0, 16, 4):
                for bi, (wn, Yt, nk) in enumerate(branches):
                    sg, vg = load_slab(WB['w_in'][l][:, O_GATE + bi * D + m0 * 128:O_GATE + bi * D + (m0 + 4) * 128], 16, 512, [("wb", 'w_in', l)])
                    sb, vb = load_slab(WB[wn][l][:, m0 * 128:(m0 + 4) * 128], nk, 512, [("wb", wn, l)])
                    for j in range(4):
                        b1 = ps_rot()
                        for kc in range(16):
                            mm(P(b1), vg[:, kc, j * 128:(j + 1) * 128], HT.ap[:, kc, :], kc == 0, kc == 15, sg.k() + HT.k(), PS.k(b1))
                        act(SGm.ap, P(b1), AF.Sigmoid, PS.k(b1), SGm.k())
                        b2 = ps_rot()
                        for kc in range(nk):
                            mm(P(b2), vb[:, kc, j * 128:(j + 1) * 128], Yt.ap[:, kc, :], kc == 0, kc == nk - 1, sb.k() + Yt.k(), PS.k(b2))
                        if bi == 0:
                            tt(MG.ap[:, j, :], P(b2), SGm.ap, ALU.mult, PS.k(b2) + SGm.k(), MG.k())
                        else:
                            tt(TMm.ap, P(b2), SGm.ap, ALU.mult, PS.k(b2) + SGm.k(), TMm.k())
                            tt(MG.ap[:, j, :], MG.ap[:, j, :], TMm.ap, ALU.add, MG.k() + TMm.k(), MG.k())
                act(MERG.ap[:, m0:m0 + 4, :], MG.ap, AF.Copy, MG.k(), MERG.k())
            for m0 in range(0, 16, 4):
                s, v = load_slab(WB['w_out'][l][:, m0 * 128:(m0 + 4) * 128], 16, 512, [("wb", 'w_out', l)])
                for j in range(4):
                    b = ps_rot()
                    for kc in range(16):
                        mm(P(b), v[:, kc, j * 128:(j + 1) * 128], MERG.ap[:, kc, :], kc == 0, kc == 15, s.k() + MERG.k(), PS.k(b))
                    stt(XT.ap[:, m0 + j, :], P(b), MOD.ap[:, 32 + m0 + j:32 + m0 + j + 1], XT.ap[:, m0 + j, :], ALU.mult, ALU.add,
                        PS.k(b) + MOD.k() + XT.k(), XT.k())
            if l == 0 and t == 0:
                dbgout("x1", XT.ap[:, 0, :], XT.k())

            mark('merge')
            norm_mod(lambda fc: WE.ap[:, 1, fc:fc + 1], lambda fc: MOD.ap[:, 48 + fc:48 + fc + 1], WE.k() + MOD.k())
            ACTB = View(AR, 0, 45056, BF16, "p (j t) -> p j t", j=44)
            CS = View(AR, 45056, 3072, F32)
            VC = View(AR, 48128, 2048, F32)
            GC = View(AR, 50176, 2048, F32)

            def conv3(b, j, out):
                conv_fm(b, FH.ap[:, j, :], FH.k(), 3, lambda kk: VEC.ap[:, V_FCW + 88 * kk + j:V_FCW + 88 * kk + j + 1],
                        VEC.ap[:, V_FCB + j:V_FCB + j + 1], out.ap, out.k())
            for j0 in range(0, 44, 4):
                sv_, vv = load_slab(WB['ffn_w_up'][l][:, j0 * 128:(j0 + 4) * 128], 16, 512, [("wb", 'ffn_w_up', l)])
                sg_, vg = load_slab(WB['ffn_w_up'][l][:, DFF + j0 * 128:DFF + (j0 + 4) * 128], 16, 512, [("wb", 'ffn_w_up', l)])
                for j in range(4):
                    b1 = ps_rot()
                    for kc in range(16):
                        mm(P(b1), vv[:, kc, j * 128:(j + 1) * 128], HT.ap[:, kc, :], kc == 0, kc == 15, sv_.k() + HT.k(), PS.k(b1))
                    conv3(b1, j0 + j, VC)
                    b2 = ps_rot()
                    for kc in range(16):
                        mm(P(b2), vg[:, kc, j * 128:(j + 1) * 128], HT.ap[:, kc, :], kc == 0, kc == 15, sg_.k() + HT.k(), PS.k(b2))
                    conv3(b2, 44 + j0 + j, GC)
                    act(GC.ap, GC.ap, AF.Silu, GC.k(), GC.k())
                    tt(ACTB.ap[:, j0 + j, :], GC.ap, VC.ap, ALU.mult, GC.k() + VC.k(), ACTB.k())
            for m0 in range(0, 16, 4):
                kcs = [(0, 16), (16, 16), (32, 12)]
                for si, (k0, kn) in enumerate(kcs):
                    s, v = load_slab(WB['ffn_w_down'][l][k0 * 128:(k0 + kn) * 128, m0 * 128:(m0 + 4) * 128], kn, 512, [("wb", 'ffn_w_down', l)])
                    for j in range(4):
                        for kc in range(kn):
                            mm(P(4 + j), v[:, kc, j * 128:(j + 1) * 128], ACTB.ap[:, k0 + kc, :], (si == 0 and kc == 0), (si == 2 and kc == kn - 1),
                               s.k() + ACTB.k(), PS.k(4 + j))
                for j in range(4):
                    stt(XT.ap[:, m0 + j, :], P(4 + j), MOD.ap[:, 80 + m0 + j:80 + m0 + j + 1], XT.ap[:, m0 + j, :], ALU.mult, ALU.add,
                        PS.k(4 + j) + MOD.k() + XT.k(), XT.k())
            if l == 0 and t == 0:
                dbgout("x2", XT.ap[:, 0, :], XT.k())

            if l < L - 1:
                sch.dma("sp", XD[:, :, t0:t0 + TT].rearrange("f p t -> p f t"), XT.ap, reads=XT.k(), writes=[("xd", t)])
            else:
                emit_output(t, final_norm)

    if L == 0:
        for t in range(NTL):
            sch.dma("sp", XT.ap, XD[:, :, t * TT:(t + 1) * TT].rearrange("f p t -> p f t"), reads=[("xd", t)], writes=XT.k())
            emit_output(t, True)

    if stop is not None:
        del sch.ops[marks[stop]:]
        for n_, ap_ in dbg_d.items():
            pass
    for o in sch.ops:
        if o.eng == "act_copy":
            raise RuntimeError
    names = ["sem_" + e for e in ENGS] + ["dma_%s_%d" % (q, i) for q in ("sp", "pool") for i in range(sch.n_dma_sems)]
    with contextlib.ExitStack() as st:
        sems = {n: st.enter_context(nc.semaphore(n)) for n in names}
        sch.finalize(sems)
        block = st.enter_context(nc.Block())
        block.sync(lambda e: sch.emit_engine("sp", e))
        block.tensor(lambda e: sch.emit_engine("pe", e))
        block.scalar(lambda e: sch.emit_engine("act", e))
        block.vector(lambda e: sch.emit_engine("dve", e))
        block.gpsimd(lambda e: sch.emit_engine("pool", e))
    return nc, len(sch.ops)


def make_in_maps(inputs, S, L, batches, l0=0, xs=None):
    hc = host_consts()
    maps = []
    for i, b in enumerate(batches):
        xin = inputs['x'][b, :S] if xs is None else xs[i]
        m = {"x": np.ascontiguousarray(xin), "c": np.ascontiguousarray(inputs['c'][b].reshape(16, 128)),
             "positions": np.ascontiguousarray(inputs['positions'][b, :S]).astype(np.int32),
             "final_norm_w": np.ascontiguousarray(inputs['final_norm_w'].reshape(16, 128))}
        if L > 0:
            for n in WSHAPES:
                m[n] = np.ascontiguousarray(inputs[n][l0:l0 + L])
        for n, v in hc.items():
            m["k_" + n] = v
        maps.append(m)
    return maps


def kernel(**inputs):
    inputs = {k: np.asarray(v) for k, v in inputs.items()}
    B, S, _ = inputs['x'].shape
    L = inputs['w_in'].shape[0]
    batches = [0, 1, 2, 3, 0, 1, 2, 3]
    nc_layer, _ = build(S, 1, final_norm=False)
    xs = None
    for l in range(L):
        maps = make_in_maps(inputs, S, 1, batches, l0=l, xs=xs)
        res = run_bass_kernel_spmd(nc_layer, maps, core_ids=list(range(8)))
        xs = [np.asarray(res.results[i]["out"]) for i in range(8)]
    nc_fin, _ = build(S, 0)
    maps = make_in_maps(inputs, S, 0, batches, xs=xs)
    res = run_bass_kernel_spmd(nc_fin, maps, core_ids=list(range(8)))
    out = np.stack([np.asarray(res.results[b]["out"]) for b in range(4)], axis=0)
    return out.astype(np.float32)
```
